# Optimizing a Trainium2 kernel written in Bass

```python
import jax
import jax.numpy as jnp
from jax import lax
import numpy as np

D_MODEL = 2048
BATCH = 2
SEQ = 4096
DEPTH = 2

D_PLE = 256
D_FF = 5632
EPS = 1e-6

SG_HEADS = 8
SG_HEAD_DIM = 128
SG_WIDTH = SG_HEADS * SG_HEAD_DIM
SG_CHUNK = 128

GLA_HEADS = 4
GLA_DK = 128
GLA_DV = 256
GLA_KW = GLA_HEADS * GLA_DK
GLA_VW = GLA_HEADS * GLA_DV
GLA_GATE_RANK = 16
GLA_GATE_TAU = 16.0
GLA_CHUNK = 64

MIX_WIDTH = SG_WIDTH + GLA_VW
IN_WIDTH = 2 * SG_WIDTH + 2 * GLA_KW + 2 * GLA_VW + GLA_GATE_RANK
MIX_SPLITS = (SG_WIDTH, 2 * SG_WIDTH, 2 * SG_WIDTH + GLA_KW, 2 * SG_WIDTH + 2 * GLA_KW,
              2 * SG_WIDTH + 2 * GLA_KW + GLA_VW, 2 * SG_WIDTH + 2 * GLA_KW + 2 * GLA_VW)

kernel_name = 'hybrid_sgmlp_gla_macaron_ple'


def rms_norm(x, g):
    xf = x.astype(jnp.float32)
    y = xf * lax.rsqrt(jnp.mean(xf * xf, axis=-1, keepdims=True) + EPS)
    return (y * g.astype(jnp.float32)).astype(x.dtype)


def layer_norm(x, g):
    xf = x.astype(jnp.float32)
    xc = xf - jnp.mean(xf, axis=-1, keepdims=True)
    y = xc * lax.rsqrt(jnp.mean(xc * xc, axis=-1, keepdims=True) + EPS)
    return (y * g.astype(jnp.float32)).astype(x.dtype)


def swiglu_ffn(x, w_in, w_out):
    gate, up = jnp.split(x @ w_in, 2, axis=-1)
    return (jax.nn.silu(gate) * up) @ w_out


def chunked_spatial_gating(u, v, v_gain, w_s, b_s):
    bsz, t_len, _ = u.shape
    n_chunks = t_len // SG_CHUNK
    u = u.reshape(bsz, n_chunks, SG_CHUNK, SG_HEADS, SG_HEAD_DIM)
    v = v.reshape(bsz, n_chunks, SG_CHUNK, SG_HEADS, SG_HEAD_DIM)
    v = layer_norm(v, v_gain.reshape(SG_HEADS, SG_HEAD_DIM))
    causal = jnp.tril(jnp.ones((SG_CHUNK, SG_CHUNK), dtype=bool))
    w = jnp.where(causal, w_s, jnp.zeros_like(w_s))
    mixed = jnp.einsum('hts,bnshd->bnthd', w, v) + b_s.T[None, None, :, :, None]
    return (u * mixed).reshape(bsz, t_len, SG_WIDTH)


def gla_chunked(q, k, v, log_a):
    out_dtype = v.dtype
    bsz, n_heads, t_len, dk = q.shape
    dv = v.shape[-1]
    n_chunks = t_len // GLA_CHUNK

    def to_chunks(t):
        t = t.astype(jnp.float32).reshape(bsz, n_heads, n_chunks, GLA_CHUNK, t.shape[-1])
        return jnp.moveaxis(t, 2, 0)

    qc, kc, vc, gc = to_chunks(q), to_chunks(k), to_chunks(v), to_chunks(log_a)
    causal = jnp.tril(jnp.ones((GLA_CHUNK, GLA_CHUNK), dtype=bool))[:, :, None]

    def step(state, inp):
        qi, ki, vi, gi = inp
        b = jnp.cumsum(gi, axis=2)
        o_inter = jnp.einsum('bhck,bhkv->bhcv', qi * jnp.exp(b), state)
        rel = jnp.where(causal, b[:, :, :, None, :] - b[:, :, None, :, :], -jnp.inf)
        scores = jnp.einsum('bhik,bhjk,bhijk->bhij', qi, ki, jnp.exp(rel))
        o = o_inter + jnp.einsum('bhij,bhjv->bhiv', scores, vi)
        b_last = b[:, :, -1:, :]
        k_dec = ki * jnp.exp(b_last - b)
        new_state = state * jnp.exp(b_last)[:, :, 0, :, None] + jnp.einsum('bhck,bhcv->bhkv', k_dec, vi)
        return new_state, o

    state0 = jnp.zeros((bsz, n_heads, dk, dv), jnp.float32)
    _, o = lax.scan(step, state0, (qc, kc, vc, gc))
    return jnp.moveaxis(o, 0, 2).reshape(bsz, n_heads, t_len, dv).astype(out_dtype)


def gla_mixer(q, k, v, r, z, w_gate, b_gate, o_gain):
    bsz, t_len, _ = q.shape
    log_a = jax.nn.log_sigmoid((z @ w_gate + b_gate).astype(jnp.float32)) / GLA_GATE_TAU

    def heads(t, d):
        return t.reshape(bsz, t_len, GLA_HEADS, d).transpose(0, 2, 1, 3)

    o = gla_chunked(heads(q * GLA_DK ** -0.5, GLA_DK), heads(k, GLA_DK),
                    heads(v, GLA_DV), heads(log_a, GLA_DK))
    o = o.transpose(0, 2, 1, 3)
    o = rms_norm(o, o_gain.reshape(GLA_HEADS, GLA_DV))
    return o.reshape(bsz, t_len, GLA_VW) * jax.nn.silu(r)


def setup_inputs(seed: int = 0) -> dict:
    key = jax.random.key(seed)
    ks = jax.random.split(key, 24)
    f32 = jnp.float32
    L, D = DEPTH, D_MODEL

    def normal(k, shape, scale):
        return jax.random.normal(k, shape, f32) * scale

    def gain(k, shape):
        return 1.0 + 0.05 * jax.random.normal(k, shape, f32)

    return {
        'x': normal(ks[0], (BATCH, SEQ, D), 1.0),
        'p': normal(ks[1], (DEPTH, BATCH, SEQ, D_PLE), 1.0),
        'ffn1_norm': gain(ks[2], (L, D)),
        'w_ffn1_in': normal(ks[3], (L, D, 2 * D_FF), D ** -0.5),
        'w_ffn1_out': normal(ks[4], (L, D_FF, D), D_FF ** -0.5),
        'mix_norm': gain(ks[5], (L, D)),
        'w_mix_in': normal(ks[6], (L, D, IN_WIDTH), D ** -0.5),
        'sg_v_gain': gain(ks[7], (L, SG_WIDTH)),
        'sg_w': normal(ks[8], (L, SG_HEADS, SG_CHUNK, SG_CHUNK), 0.5 * SG_CHUNK ** -0.5),
        'sg_b': 1.0 + normal(ks[9], (L, SG_HEADS, SG_CHUNK), 0.1),
        'gla_w_gate': normal(ks[10], (L, GLA_GATE_RANK, GLA_KW), GLA_GATE_RANK ** -0.5),
        'gla_b_gate': normal(ks[11], (L, GLA_KW), 0.1),
        'gla_o_gain': gain(ks[12], (L, GLA_VW)),
        'w_mix_out': normal(ks[13], (L, MIX_WIDTH, D), MIX_WIDTH ** -0.5),
        'ffn2_norm': gain(ks[14], (L, D)),
        'w_ffn2_in': normal(ks[15], (L, D, 2 * D_FF), D ** -0.5),
        'w_ffn2_out': normal(ks[16], (L, D_FF, D), D_FF ** -0.5),
        'ple_norm': gain(ks[17], (L, D)),
        'w_ple_gate': normal(ks[18], (L, D, D), D ** -0.5),
        'w_ple_proj': normal(ks[19], (L, D_PLE, D), D_PLE ** -0.5),
        'final_norm': gain(ks[20], (D,)),
    }


def reference(x, p, ffn1_norm, w_ffn1_in, w_ffn1_out, mix_norm, w_mix_in, sg_v_gain, sg_w, sg_b,
              gla_w_gate, gla_b_gate, gla_o_gain, w_mix_out, ffn2_norm, w_ffn2_in, w_ffn2_out,
              ple_norm, w_ple_gate, w_ple_proj, final_norm):
    h = x
    for i in range(DEPTH):
        h = h + 0.5 * swiglu_ffn(rms_norm(h, ffn1_norm[i]), w_ffn1_in[i], w_ffn1_out[i])
        n = rms_norm(h, mix_norm[i])
        a_u, a_v, q, k, v, r, z = jnp.split(n @ w_mix_in[i], MIX_SPLITS, axis=-1)
        y_a = chunked_spatial_gating(jax.nn.gelu(a_u, approximate=False),
                                     jax.nn.gelu(a_v, approximate=False),
                                     sg_v_gain[i], sg_w[i], sg_b[i])
        y_b = gla_mixer(q, k, v, r, z, gla_w_gate[i], gla_b_gate[i], gla_o_gain[i])
        h = h + jnp.concatenate([y_a, y_b], axis=-1) @ w_mix_out[i]
        h = h + 0.5 * swiglu_ffn(rms_norm(h, ffn2_norm[i]), w_ffn2_in[i], w_ffn2_out[i])
        gate = jax.nn.sigmoid(rms_norm(h, ple_norm[i]) @ w_ple_gate[i])
        h = h + gate * (p[i] @ w_ple_proj[i])
    return rms_norm(h, final_norm)
```

```python
import numpy as np
import concourse.bass as bass
import concourse.mybir as mybir
from concourse.bass_utils import run_bass_kernel_spmd

F32 = mybir.dt.float32
BF16 = mybir.dt.bfloat16
AF = mybir.ActivationFunctionType
ALU = mybir.AluOpType
AX = mybir.AxisListType

NCORES = 8
D = 2048
KC = 16
NT = 1024
TH = 2
NTT = 8
DFF = 5632
NFC = 44
GC = 4
NG = 11
DEPTH = 2
INW = 5136
EPS = 1e-6
NPAR = 9 * 16 + 16 + 8
PAYW = 1028


class Plan:
    CE = ("pe", "act", "dve", "pool")

    def __init__(self):
        self.ops = {e: [] for e in ("pe", "act", "dve", "pool", "sp")}
        self.cnt = {e: 0 for e in self.CE}
        self.pend = {e: False for e in self.CE}
        self.waited = {e: {} for e in self.ops}
        self.res = {}
        self.dcnt = {}
        self.muted = False

    def _res(self, k):
        r = self.res.get(k)
        if r is None:
            r = self.res[k] = [None, {}]
        return r

    def snapshot(self):
        return [(e, self.cnt[e] + (1 if self.pend[e] else 0)) for e in self.CE
                if self.cnt[e] or self.pend[e]]

    def guard(self, keys, toks=None):
        if self.muted:
            return
        if toks is None:
            toks = self.snapshot()
        for k in keys:
            R = self._res(k)
            for (s, v) in toks:
                if R[1].get(s, 0) < v:
                    R[1][s] = v

    def emit(self, e, fn, reads=(), writes=(), extra=(), signal=True, dma=None, dma_inc=16):
        if self.muted:
            return None
        deps = {}
        own_raw = 0

        def add(tok, raw=False):
            nonlocal own_raw
            if tok is None:
                return
            k, v = tok
            if k == e:
                if raw and v > own_raw:
                    own_raw = v
                return
            if v > deps.get(k, 0):
                deps[k] = v

        for r in reads:
            R = self._res(r)
            add(R[0], True)
        for w in writes:
            R = self._res(w)
            add(R[0])
            for k, v in R[1].items():
                add((k, v))
        for t in extra:
            add(t, True)
        wd = self.waited[e]
        waits = []
        for k, v in deps.items():
            if wd.get(k, 0) < v:
                wd[k] = v
                waits.append((k, v))
        if own_raw and e in ("act", "dve", "pool") and wd.get(e, 0) < own_raw:
            wd[e] = own_raw
            waits.append((e, own_raw))
        if dma is not None:
            self.dcnt[dma] = self.dcnt.get(dma, 0) + dma_inc
            tok = (dma, self.dcnt[dma])
            inc = (dma, dma_inc)
        elif signal:
            self.cnt[e] += 1
            self.pend[e] = False
            tok = (e, self.cnt[e])
            inc = (e, 1)
        else:
            self.pend[e] = True
            tok = (e, self.cnt[e] + 1)
            inc = None
        k, v = tok
        for r in reads:
            R = self._res(r)
            if R[1].get(k, 0) < v:
                R[1][k] = v
        for w in writes:
            R = self._res(w)
            R[0] = tok
            R[1] = {}
        self.ops[e].append((waits, fn, inc))
        return tok


def build_program(stop_after=None, phase=None, seq=False):
    nc = bass.Bass("TRN2", target_bir_lowering=False)
    P = Plan()
    dt = nc.dram_tensor
    needed = {}
    outs_decl = {}

    dummies = {}

    def din(name, shape, dtype=F32):
        if name in needed:
            return needed[name]
        if P.muted:
            if name not in dummies:
                dummies[name] = dt("dmy_" + name, list(shape), dtype).ap()
            return dummies[name]
        needed[name] = dt(name, list(shape), dtype, kind="ExternalInput").ap()
        return needed[name]

    def dout(name, shape, dtype=F32):
        if name not in outs_decl:
            outs_decl[name] = dt(name, list(shape), dtype, kind="ExternalOutput").ap()
        return outs_decl[name]

    LSHAPES = {
        "w_ffn1_in": [D, 2 * DFF], "w_ffn1_out": [DFF, D], "w_mix_in": [D, INW], "w_mix_out": [D, D],
        "w_ffn2_in": [D, 2 * DFF], "w_ffn2_out": [DFF, D], "w_ple_gate": [D, D], "w_ple_proj": [256, D],
        "sg_v_gain": [1, 1024], "sg_wT": [128, 1024], "sg_b": [1, 1024], "gla_w_gate": [16, 512],
        "gla_b_gate": [1, 512], "pT": [256, NT],
    }

    def wl(name, l):
        return din(f"{name}_{l}", LSHAPES[name])

    NSEG = 4 if seq else 1
    SEQT = NT * NSEG
    if seq:
        LSHAPES["pT"] = [256, SEQT]
        gla_state = dt("gla_state", [DEPTH, 128, 1024], F32)
    cur_seg = [0]
    fused = (phase is None) and not seq
    if fused:
        cc_in = dt("cc_in", [128, PAYW], F32)
        cc_out = dt("cc_out", [NCORES * 128, PAYW], F32)

    sb = nc.sbuf_tensor
    import contextlib
    es = contextlib.ExitStack()
    with es:
        ent = es.enter_context
        h = ent(sb("h", [128, KC, NT], F32))
        nb = ent(sb("nb", [128, KC, NT], BF16))
        aslots = ent(sb("aslots", [128, 3, KC * 256], BF16))
        R2 = ent(sb("R2", [128, 16384], BF16))
        R1 = ent(sb("R1", [128, 20480], BF16))
        ones_bf = ent(sb("ones_bf", [128, 128], BF16))
        U_inc = ent(sb("U_inc", [128, 128], BF16))
        U_str = ent(sb("U_str", [128, 128], BF16))
        maskT = ent(sb("maskT", [128, 128], F32))
        epsT = ent(sb("epsT", [128, 1], F32))
        parT = ent(sb("parT", [128, NPAR], F32))
        wTm = ent(sb("wTm", [128, 8, 128], BF16))
        bs_row = ent(sb("bs_row", [1, 1024], BF16))
        wgate = ent(sb("wgate", [17, 512], BF16))
        vgain = ent(sb("vgain", [128, 1024], F32))
        small = ent(sb("small", [128, 256], F32))
        psb = [ent(nc.psum_tensor(f"ps{i}", [128, 512], F32)) for i in range(8)]
        sem_names = ["pe", "act", "dve", "pool", "xin", "par", "lwT", "lbs", "lwg", "lvg", "pin", "A0", "A1", "A2",
                     "B0", "B1", "cci", "cc", "stg", "out", "rst", "gst"]
        sems = {n: ent(nc.semaphore(n)) for n in sem_names}
        block = ent(nc.Block())

        def carve(reg, off, nbytes, dtype=BF16, pat=None, **kw):
            v = reg[:, off // 2:(off + nbytes) // 2]
            if dtype == F32:
                v = v.bitcast(F32)
            if pat:
                v = v.rearrange(pat, **kw)
            return v

        KB = 1024
        actb = [carve(R1, i * 8 * KB, 8 * KB, BF16, "p (c t) -> p c t", c=GC) for i in range(2)]
        ftmp = [carve(R1, 16 * KB + i * 2 * KB, 2 * KB, F32) for i in range(4)]
        wout = [carve(R2, i * 16 * KB, 16 * KB, BF16, "p (c d) -> p c d", c=GC) for i in range(2)]
        u_g = carve(R1, 0, 16 * KB, BF16, "p (c t) -> p c t", c=8)
        v_ln = carve(R1, 16 * KB, 16 * KB, BF16, "p (t c) -> p t c", t=NTT)
        vgt = [carve(R1, 32 * KB + i * KB, KB, F32) for i in range(4)]
        sqt = [carve(R1, 36 * KB + i * KB, KB, F32) for i in range(2)]
        qd = carve(R1, 0, 8 * KB, BF16, "p (c t) -> p c t", c=4)
        kd = carve(R1, 8 * KB, 8 * KB, BF16, "p (c t) -> p c t", c=4)
        kd_tok = carve(R1, 16 * KB, 8 * KB, BF16, "p (t c) -> p t c", t=NTT)
        v_tok = carve(R1, 24 * KB, 16 * KB, BF16, "p (t c) -> p t c", t=NTT)
        sp_hi = carve(R2, 0, 8 * KB, BF16, "p (t c) -> p t c", t=NTT)
        sp_lo = carve(R2, 8 * KB, 8 * KB, BF16, "p (t c) -> p t c", t=NTT)
        zT = carve(R2, 16 * KB, 2 * KB, BF16)
        mtmp = [carve(R2, 18 * KB + i * 2 * KB, 2 * KB, F32) for i in range(3)]
        r_s = carve(R2, 0, 16 * KB, BF16, "p (c t) -> p c t", c=8)
        S_bf = [carve(R2, 16 * KB + i * 2 * KB, 2 * KB, BF16) for i in range(2)]
        p2f = [carve(R2, 20 * KB + i * 512, 512, F32) for i in range(4)]
        p2b = [carve(R2, 22 * KB + i * 512, 512, BF16) for i in range(2)]
        scm = [carve(R2, 23 * KB + i * 256, 256, BF16) for i in range(2)]
        stg = carve(R2, 23 * KB + 512, 4128, F32)
        Sst = carve(R2, 28 * KB - 32, 4128, F32)
        pTb = carve(R1, 24 * KB, 4 * KB, BF16, "p (k t) -> p k t", k=2)

        def aslot(i, pat=None, **kw):
            v = aslots[:, i, :]
            if pat:
                v = v.rearrange(pat, **kw)
            return v

        blast = small[:, 32:64].rearrange("p (h t) -> p h t", h=4)
        dec = small[:, 64:96].rearrange("p (h t) -> p h t", h=4)
        ea = small[:, 96:100]
        a3 = small[:, 100:104]

        astate = {"i": 0, "keep": None}

        def next_aslot():
            while True:
                i = astate["i"] % 3
                astate["i"] += 1
                if i != astate["keep"]:
                    return i

        def mm(out, lhsT, rhs, start, stop, reads, writes, extra=(), sig=False):
            return P.emit("pe", lambda e: e.matmul(out, lhsT=lhsT, rhs=rhs, start=start, stop=stop),
                          reads=reads, writes=writes, extra=extra, signal=bool(stop) or sig)

        def act(out, in_, func, reads, writes, bias=None, scale=None):
            kw = {}
            if bias is not None:
                kw["bias"] = bias
            if scale is not None:
                kw["scale"] = scale
            return P.emit("act", lambda e: e.activation(out=out, in_=in_, func=func, **kw),
                          reads=reads, writes=writes)

        def dve(fn, reads, writes):
            return P.emit("dve", fn, reads=reads, writes=writes)

        def wdma(out, in_, writes):
            slotkey = writes[0]
            semn = slotkey[0] + str(slotkey[1])
            return P.emit("pool", lambda e: e.dma_start(out=out, in_=in_), writes=writes, dma=semn)

        def spdma(out, in_, sem, reads=(), writes=(), extra=()):
            return P.emit("sp", lambda e: e.dma_start(out=out, in_=in_), reads=reads, writes=writes,
                          extra=extra, dma=sem)

        def hk(k, th):
            return ("h", k, th)

        def nbk(k, th):
            return ("nb", k, th)

        def pool_op(fn, writes, reads=()):
            return P.emit("pool", fn, reads=reads, writes=writes)

        pool_op(lambda e: e.memset(small[:], 0.0), [("sm", "blast"), ("sm", "dec"), ("sm", "ea"), ("sm", "a3")])
        pool_op(lambda e: e.memset(ones_bf[:], 1.0), [("c", "ones")])
        pool_op(lambda e: e.memset(epsT[:], EPS), [("c", "eps")])
        pool_op(lambda e: e.memset(U_inc[:], -1.0 / 16.0), [("c", "Ui")])
        pool_op(lambda e: e.affine_select(out=U_inc[:], in_=U_inc[:], pattern=[[1, 128]],
                                          compare_op=ALU.is_ge, fill=0.0, base=0, channel_multiplier=-1),
                [("c", "Ui")], [("c", "Ui")])
        pool_op(lambda e: e.memset(U_str[:], -1.0 / 16.0), [("c", "Us")])
        pool_op(lambda e: e.affine_select(out=U_str[:], in_=U_str[:], pattern=[[-1, 128]],
                                          compare_op=ALU.is_gt, fill=0.0, base=0, channel_multiplier=1),
                [("c", "Us")], [("c", "Us")])
        pool_op(lambda e: e.memset(maskT[:], 1.0), [("c", "mk")])
        pool_op(lambda e: e.affine_select(out=maskT[:], in_=maskT[:], pattern=[[1, 128]],
                                          compare_op=ALU.is_ge, fill=0.0, base=0, channel_multiplier=-1),
                [("c", "mk")], [("c", "mk")])
        CONST = [("c", "ones"), ("c", "eps"), ("c", "Ui"), ("c", "Us"), ("c", "mk"), ("c", "par")]

        spdma(parT[:], din("par", [128, NPAR])[:, :], "par", writes=[("c", "par")])
        def load_x(seg=0):
            xv = din("xT", [D, SEQT]).rearrange("(k p) t -> p k t", p=128)
            for k in range(KC):
                spdma(h[:, k, :], xv[:, k, seg * NT:(seg + 1) * NT], "xin", writes=[hk(k, 0), hk(k, 1)])
            for k in range(KC):
                for th in range(TH):
                    P.res[hk(k, th)][0] = ("xin", 16 * KC * (seg + 1))

        def norm_stage(gcol, final=False):
            for th in range(TH):
                ts = slice(th * 512, (th + 1) * 512)
                bank = 6 + th
                for k in range(KC):
                    t = ftmp[k % 2].bitcast(BF16)[:, 0:512]
                    tk = ("ftmp", k % 2)
                    act(t, h[:, k, ts], AF.Square, [hk(k, th)], [tk])
                    mm(psb[bank][:], ones_bf[:], t, k == 0, k == KC - 1,
                       [tk, ("c", "ones")], [("ps", bank)], sig=True)
                rt = ftmp[2 + th]
                rk = ("ftmp", 2 + th)
                act(rt, psb[bank][:], AF.Ln, [("ps", bank), ("c", "eps")], [rk], bias=epsT[:], scale=1.0 / D)
                act(rt, rt, AF.Exp, [rk], [rk], scale=-0.5)
                for k in range(KC):
                    g = parT[:, gcol + k:gcol + k + 1]
                    if final:
                        o = h[:, k, ts]
                        wr = [hk(k, th)]
                    else:
                        o = nb[:, k, ts]
                        wr = [nbk(k, th)]
                    dve(lambda e, o=o, i0=h[:, k, ts], g=g, rt=rt: e.scalar_tensor_tensor(
                        out=o, in0=i0, scalar=g, in1=rt, op0=ALU.mult, op1=ALU.mult),
                        [hk(k, th), rk, ("c", "par")], wr)

        def ffn_stage(w_in, w_out):
            wiv = w_in.rearrange("(k p) c -> p k c", p=128)
            wov = w_out.rearrange("(c p) d -> p c d", p=128)
            ti = [0]

            def p1(g):
                ab = actb[g % 2]
                for fc in range(GC):
                    f = g * GC + fc
                    si = next_aslot()
                    sl = aslot(si, "p (k c) -> p k c", k=KC)
                    sk = ("A", si)
                    wdma(sl[:, :, 0:128], wiv[:, :, f * 128:(f + 1) * 128], [sk])
                    wdma(sl[:, :, 128:256], wiv[:, :, DFF + f * 128:DFF + (f + 1) * 128], [sk])
                    for half, banks in ((0, (0, 1)), (1, (2, 3))):
                        for k in range(KC):
                            for th in range(TH):
                                mm(psb[banks[th]][:], sl[:, k, half * 128:(half + 1) * 128],
                                   nb[:, k, th * 512:(th + 1) * 512], k == 0, k == KC - 1,
                                   [sk, nbk(k, th)], [("ps", banks[th])])
                    for th in range(TH):
                        t = ftmp[ti[0] % 4]
                        tk = ("ftmp", ti[0] % 4)
                        ti[0] += 1
                        act(t, psb[th][:], AF.Silu, [("ps", th)], [tk])
                        dve(lambda e, o=ab[:, fc, th * 512:(th + 1) * 512], i0=psb[2 + th][:], t=t:
                            e.tensor_tensor(out=o, in0=i0, in1=t, op=ALU.mult),
                            [("ps", 2 + th), tk], [("actb", g % 2, fc, th)])

            def p2(g):
                ab = actb[g % 2]
                bi = g % 2
                bk = ("B", bi)
                for j in range(KC):
                    banks = (4, 5) if j % 2 == 0 else (6, 7)
                    for fc in range(GC):
                        for th in range(TH):
                            mm(psb[banks[th]][:], wout[bi][:, fc, j * 128:(j + 1) * 128],
                               ab[:, fc, th * 512:(th + 1) * 512], fc == 0, fc == GC - 1,
                               [bk, ("actb", g % 2, fc, th)], [("ps", banks[th])])
                    for th in range(TH):
                        hs = h[:, j, th * 512:(th + 1) * 512]
                        dve(lambda e, hs=hs, ps=psb[banks[th]][:]: e.scalar_tensor_tensor(
                            out=hs, in0=ps, scalar=0.5, in1=hs, op0=ALU.mult, op1=ALU.add),
                            [("ps", banks[th]), hk(j, th)], [hk(j, th)])

            def loadB(g):
                wdma(wout[g % 2][:], wov[:, g * GC:(g + 1) * GC, :], [("B", g % 2)])

            p1(0)
            loadB(0)
            for g in range(1, NG):
                p1(g)
                p2(g - 1)
                if g < NG:
                    loadB(g)
            p2(NG - 1)

        def proj_fm(wv, c0, nchunks, evac, nk=KC, rhs_fn=None, rkey_fn=None):
            if rhs_fn is None:
                rhs_fn = lambda k, th: nb[:, k, th * 512:(th + 1) * 512]
                rkey_fn = nbk
            cpp = (KC * 256) // (nk * 128)
            for s0 in range(0, nchunks, cpp):
                n = min(cpp, nchunks - s0)
                si = next_aslot()
                sl = aslot(si, "p (k c) -> p k c", k=nk)
                sk = ("A", si)
                wdma(sl[:, :, 0:n * 128], wv[:, :, c0 + s0 * 128:c0 + (s0 + n) * 128], [sk])
                for j in range(n):
                    fc = s0 + j
                    banks = (0, 1) if fc % 2 == 0 else (2, 3)
                    for k in range(nk):
                        for th in range(TH):
                            mm(psb[banks[th]][:], sl[:, k, j * 128:(j + 1) * 128], rhs_fn(k, th),
                               k == 0, k == nk - 1, [sk, rkey_fn(k, th)], [("ps", banks[th])])
                    evac(fc, banks)

        def proj_tok(wv, c0, nslots, evac):
            for q in range(nslots):
                si = next_aslot()
                sl = aslot(si, "p (k c) -> p k c", k=KC)
                sk = ("A", si)
                wdma(sl[:, :, :], wv[:, :, c0 + q * 256:c0 + (q + 1) * 256], [sk])
                for tt in range(NTT):
                    bank = 4 + (tt % 2)
                    pk = ("ps", bank)
                    o = psb[bank][:, 0:256]
                    th = tt // 4
                    for k in range(KC):
                        mm(o, nb[:, k, tt * 128:(tt + 1) * 128], sl[:, k, :], k == 0, k == KC - 1,
                           [sk, nbk(k, th)], [pk])
                    evac(q, tt, o, pk)

        def mix_out_half(l, r0, ybuf, ykey):
            wv = wl("w_mix_out", l)[r0:r0 + 1024, :].rearrange("(k p) c -> p k c", p=128)

            def evac(fc, banks):
                for th in range(TH):
                    hs = h[:, fc, th * 512:(th + 1) * 512]
                    dve(lambda e, hs=hs, ps=psb[banks[th]][:]: e.tensor_tensor(
                        out=hs, in0=ps, in1=hs, op=ALU.add),
                        [("ps", banks[th]), hk(fc, th)], [hk(fc, th)])

            proj_fm(wv, 0, KC, evac, nk=8,
                    rhs_fn=lambda k, th: ybuf[:, k, th * 512:(th + 1) * 512],
                    rkey_fn=lambda k, th: (ykey, k, th))

        def mark(name):
            if stop_after == name:
                P.muted = True

        def mixer_stage(l, part=0):
            P.muted = (part == 2)
            wv = wl("w_mix_in", l).rearrange("(k p) c -> p k c", p=128)
            d_wT, d_sb, d_wg, d_bg = wl("sg_wT", l), wl("sg_b", l), wl("gla_w_gate", l), wl("gla_b_gate", l)
            P.emit("pool", lambda e: e.dma_start(out=wTm[:].rearrange("p h t -> p (h t)"), in_=d_wT),
                   writes=[("c", "wTm")], dma="lwT")
            P.emit("pool", lambda e: e.dma_start(out=bs_row[:], in_=d_sb),
                   writes=[("c", "bs")], dma="lbs")
            P.emit("pool", lambda e: e.dma_start(out=wgate[0:16, :], in_=d_wg),
                   writes=[("c", "wg")], dma="lwg")
            P.emit("pool", lambda e: e.dma_start(out=wgate[16:17, :], in_=d_bg),
                   writes=[("c", "wg")], dma="lwg")
            P.emit("pool", lambda e: e.affine_select(out=wTm[:], in_=wTm[:], pattern=[[0, 8], [1, 128]],
                                                     compare_op=ALU.is_ge, fill=0.0, base=0,
                                                     channel_multiplier=-1),
                   reads=[("c", "wTm")], writes=[("c", "wTm")])
            spdma(vgain[:], wl("sg_v_gain", l).broadcast_to([128, 1024]), "lvg", writes=[("c", "vg")])

            sgkeys = [("ug", c, th) for c in range(8) for th in range(TH)] + \
                     [("vln", tt) for tt in range(NTT)] + [("vgt", i) for i in range(4)] + \
                     [("sqt", i) for i in range(2)]
            P.guard(sgkeys)

            def evac_u(fc, banks):
                for th in range(TH):
                    act(u_g[:, fc, th * 512:(th + 1) * 512], psb[banks[th]][:], AF.Gelu,
                        [("ps", banks[th])], [("ug", fc, th)])

            proj_fm(wv, 0, 8, evac_u)

            vi = [0]

            def evac_vsg(q, tt, ps, pk):
                i = vi[0] % 4
                vi[0] += 1
                vg = vgt[i]
                vk = ("vgt", i)
                sm0 = 128 + i * 20
                sm_stat = small[:, sm0:sm0 + 12]
                sm_mv = small[:, sm0 + 12:sm0 + 16]
                sm_rs = small[:, sm0 + 16:sm0 + 18]
                sm_ln = small[:, sm0 + 18:sm0 + 20]
                kst, kmv, krs_, kln = ("smst", i), ("smmv", i), ("smrs", i), ("smln", i)
                act(vg, ps, AF.Gelu, [pk], [vk])
                for hh in range(2):
                    dve(lambda e, hh=hh, vg=vg, sm_stat=sm_stat: e.bn_stats(
                        out=sm_stat[:, hh * 6:(hh + 1) * 6], in_=vg[:, hh * 128:(hh + 1) * 128]),
                        [vk], [kst])
                    dve(lambda e, hh=hh, sm_stat=sm_stat, sm_mv=sm_mv: e.bn_aggr(
                        out=sm_mv[:, hh * 2:(hh + 1) * 2], in_=sm_stat[:, hh * 6:(hh + 1) * 6]),
                        [kst], [kmv])
                mvv = sm_mv.rearrange("p (h c) -> p h c", c=2)
                act(sm_ln.unsqueeze(2), mvv[:, :, 1:2], AF.Ln, [kmv, ("c", "eps")], [kln],
                    bias=epsT[:], scale=1.0)
                act(sm_rs, sm_ln, AF.Exp, [kln], [krs_], scale=-0.5)
                for hh in range(2):
                    dve(lambda e, hh=hh, vg=vg, sm_mv=sm_mv, sm_rs=sm_rs: e.tensor_scalar(
                        out=vg[:, hh * 128:(hh + 1) * 128], in0=vg[:, hh * 128:(hh + 1) * 128],
                        scalar1=sm_mv[:, hh * 2:hh * 2 + 1], scalar2=sm_rs[:, hh:hh + 1],
                        op0=ALU.subtract, op1=ALU.mult),
                        [vk, kmv, krs_], [vk])
                dve(lambda e, vg=vg, q=q, tt=tt: e.tensor_tensor(
                    out=v_ln[:, tt, q * 256:(q + 1) * 256], in0=vg, in1=vgain[:, q * 256:(q + 1) * 256],
                    op=ALU.mult),
                    [vk, ("c", "vg")], [("vln", tt)])

            proj_tok(wv, 1024, 4, evac_vsg)

            mi = [0]
            for tt in range(NTT):
                th = tt // 4
                for hd in range(8):
                    r = mi[0] % 4
                    mi[0] += 1
                    pm = psb[4 + r][:, 0:128]
                    pk = ("ps", 4 + r)
                    mm(pm, v_ln[:, tt, hd * 128:(hd + 1) * 128], wTm[:, hd, :], True, False,
                       [("vln", tt), ("c", "wTm")], [pk])
                    mm(pm, ones_bf[0:1, :], bs_row[0:1, hd * 128:(hd + 1) * 128], False, True,
                       [("c", "ones"), ("c", "bs")], [pk])
                    us = u_g[:, hd, tt * 128:(tt + 1) * 128]
                    dve(lambda e, us=us, pm=pm: e.tensor_tensor(out=us, in0=pm, in1=us, op=ALU.mult),
                        [pk, ("ug", hd, th)], [("ug", hd, th)])
            mix_out_half(l, 0, u_g, "ug")
            if stop_after == ("mixA", l):
                return True

            mark("sg")
            m1keys = [("qd", c, th) for c in range(4) for th in range(TH)] + \
                     [("kd", c, th) for c in range(4) for th in range(TH)] + \
                     [("kdt", tt) for tt in range(NTT)] + [("vt", tt) for tt in range(NTT)] + \
                     [("sp", tt) for tt in range(NTT)] + [("spl", tt) for tt in range(NTT)] + [("zT",), ("mtmp", 0), ("mtmp", 1), ("mtmp2", 0),
                                                          ("mtmp2", 1), ("S", 0), ("S", 1), ("S", 2),
                                                          ("S", 3), ("S", "B"), ("stg",)]
            P.guard(m1keys)
            dve(lambda e: e.memset(zT[0:32, :], 1.0), [], [("zT",)])
            si = next_aslot()
            slz = aslot(si, "p (k c) -> p k c", k=KC)
            skz = ("A", si)
            wdma(slz[:, :, 0:16], wv[:, :, 5120:5136], [skz])
            for k in range(KC):
                for th in range(TH):
                    mm(psb[th][0:16, :], slz[:, k, 0:16], nb[:, k, th * 512:(th + 1) * 512],
                       k == 0, k == KC - 1, [skz, nbk(k, th)], [("ps", th)])
            for th in range(TH):
                act(zT[0:16, th * 512:(th + 1) * 512], psb[th][0:16, :], AF.Copy, [("ps", th)], [("zT",)])
            mark("z")
            for tt in range(NTT):
                bank = 2 + (tt % 2)
                pk = ("ps", bank)
                mm(psb[bank][:], zT[0:17, tt * 128:(tt + 1) * 128], wgate[0:17, :], True, True,
                   [("zT",), ("c", "wg")], [pk])
                t = mtmp[tt % 2]
                tk = ("mtmp", tt % 2)
                act(t, psb[bank][:], AF.Exp, [pk], [tk], scale=-1.0)
                act(t, t, AF.Ln, [tk], [tk], bias=1.0, scale=1.0)
                dve(lambda e, tt=tt, t=t: e.tensor_copy(out=sp_hi[:, tt, :], in_=t), [tk], [("sp", tt)])
                dve(lambda e, tt=tt, t=t: e.tensor_tensor(out=sp_lo[:, tt, :], in0=t, in1=sp_hi[:, tt, :],
                                                          op=ALU.subtract),
                    [tk, ("sp", tt)], [("spl", tt)])

            mark("sp")

            def make_evac_qk(dst, dkey, is_q):
                def evac(fc, banks):
                    hd = fc
                    for th in range(TH):
                        bb = 4 + th
                        for t4 in range(4):
                            tt = th * 4 + t4
                            mm(psb[bb][:, t4 * 128:(t4 + 1) * 128], sp_hi[:, tt, hd * 128:(hd + 1) * 128],
                               U_inc[:], True, False, [("sp", tt), ("c", "Ui")], [("ps", bb)])
                            mm(psb[bb][:, t4 * 128:(t4 + 1) * 128], sp_lo[:, tt, hd * 128:(hd + 1) * 128],
                               U_inc[:], False, True, [("spl", tt), ("c", "Ui")], [("ps", bb)])
                        t = mtmp[th]
                        tk = ("mtmp", th)
                        if is_q:
                            dve(lambda e, hd=hd, th=th, bb=bb: e.tensor_copy(
                                out=blast[:, hd, th * 4:(th + 1) * 4],
                                in_=psb[bb][:].rearrange("p (t c) -> p t c", c=128)[:, :, 127]),
                                [("ps", bb)], [("sm", "blast")])
                            act(t, psb[bb][:], AF.Exp, [("ps", bb)], [tk])
                            dve(lambda e, o=dst[:, hd, th * 512:(th + 1) * 512], ps=psb[banks[th]][:], t=t:
                                e.scalar_tensor_tensor(out=o, in0=ps, scalar=128.0 ** -0.5, in1=t,
                                                       op0=ALU.mult, op1=ALU.mult),
                                [("ps", banks[th]), tk], [(dkey, hd, th)])
                        else:
                            act(t, psb[bb][:], AF.Exp, [("ps", bb)], [tk], scale=-1.0)
                            dve(lambda e, o=dst[:, hd, th * 512:(th + 1) * 512], ps=psb[banks[th]][:], t=t:
                                e.tensor_tensor(out=o, in0=ps, in1=t, op=ALU.mult),
                                [("ps", banks[th]), tk], [(dkey, hd, th)])
                return evac

            proj_fm(wv, 2048, 4, make_evac_qk(qd, "qd", True))
            proj_fm(wv, 2560, 4, make_evac_qk(kd, "kd", False))
            act(dec.rearrange("p h t -> p (h t)"), blast.rearrange("p h t -> p (h t)"), AF.Exp,
                [("sm", "blast")], [("sm", "dec")])

            mark("qk")

            def evac_ktok(q, tt, ps, pk):
                pd = psb[6 + (tt % 2)][:, 0:256]
                pdk = ("ps", 6 + (tt % 2))
                mm(pd, U_str[:], sp_hi[:, tt, q * 256:(q + 1) * 256], True, False,
                   [("sp", tt), ("c", "Us")], [pdk])
                mm(pd, U_str[:], sp_lo[:, tt, q * 256:(q + 1) * 256], False, True,
                   [("spl", tt), ("c", "Us")], [pdk])
                t = mtmp[2][:, (tt % 2) * 256:(tt % 2 + 1) * 256]
                tk = ("mtmp2", tt % 2)
                act(t, pd, AF.Exp, [pdk], [tk])
                dve(lambda e, q=q, tt=tt, ps=ps, t=t: e.tensor_tensor(
                    out=kd_tok[:, tt, q * 256:(q + 1) * 256], in0=ps, in1=t, op=ALU.mult),
                    [pk, tk], [("kdt", tt)])

            proj_tok(wv, 2560, 2, evac_ktok)

            mark("ktok")

            def evac_vtok(q, tt, ps, pk):
                act(v_tok[:, tt, q * 256:(q + 1) * 256], ps, AF.Copy, [pk], [("vt", tt)])

            proj_tok(wv, 3072, 4, evac_vtok)

            mark("vtok")
            Sv = Sst[:, 0:1024].rearrange("p (h v) -> p h v", h=4)

            def state_update(tt, first):
                for hd in range(4):
                    bank = hd % 2
                    pS = psb[bank][:, 0:256]
                    pk = ("ps", bank)
                    mm(pS, kd_tok[:, tt, hd * 128:(hd + 1) * 128], v_tok[:, tt, hd * 256:(hd + 1) * 256],
                       True, True, [("kdt", tt), ("vt", tt)], [pk])
                    if first:
                        dve(lambda e, hd=hd, pS=pS: e.tensor_copy(out=Sv[:, hd, :], in_=pS),
                            [pk], [("S", hd)])
                    else:
                        dve(lambda e, hd=hd, pS=pS, tt=tt: e.scalar_tensor_tensor(
                            out=Sv[:, hd, :], in0=Sv[:, hd, :], scalar=dec[:, hd, tt:tt + 1], in1=pS,
                            op0=ALU.mult, op1=ALU.add),
                            [pk, ("S", hd), ("sm", "dec")], [("S", hd)])

            if not seq:
                for tt in range(NTT):
                    state_update(tt, tt == 0)
                dve(lambda e: e.tensor_reduce(out=Sst[:, 1024:1028], in_=blast, axis=AX.X, op=ALU.add),
                    [("sm", "blast")], [("S", "B")])
            skeys = [("S", hd) for hd in range(4)] + [("S", "B")]
            if seq:
                pass
            elif fused:
                spdma(cc_in[:, :], Sst[:, 0:PAYW], "cci", reads=skeys, writes=[("ccin",)])
                P.emit("pool", lambda e: e.collective_compute(
                    "AllGather", ALU.bypass, replica_groups=[list(range(NCORES))],
                    ins=[cc_in[:, :].opt()], outs=[cc_out[:, :].opt()]),
                    reads=[("ccin",)], writes=[("ccout",)], dma="cc", dma_inc=1)
                gsrc = cc_out
            else:
                if part == 1:
                    spdma(dout("payload", [128, PAYW])[:, :], Sst[:, 0:PAYW], "out", reads=skeys)
                P.muted = (part == 1)
                gsrc = din("gath", [NCORES * 128, PAYW])
                wv = wl("w_mix_in", l).rearrange("(k p) c -> p k c", p=128)

            P.guard([("rs", c, th) for c in range(8) for th in range(TH)])

            def evac_r(fc, banks):
                for th in range(TH):
                    act(r_s[:, fc, th * 512:(th + 1) * 512], psb[banks[th]][:], AF.Silu,
                        [("ps", banks[th])], [("rs", fc, th)])

            proj_fm(wv, 4096, 8, evac_r)

            P.guard([("Sbf", 0), ("Sbf", 1), ("stg",)] + [(n, i) for n in ("p2rs", "p2t", "osq", "scm") for i in range(2)])
            if seq:
                if cur_seg[0] == 0:
                    dve(lambda e: e.memset(Sst[:, 0:1024], 0.0), [], skeys)
                else:
                    spdma(Sst[:, 0:1024], gla_state[l], "stg", reads=[("gst", l)], writes=skeys)
            else:
                dve(lambda e: e.memset(Sst[:, 0:1024], 0.0), [], skeys)
            for j in (() if seq else (0, 1, 2, 4, 5, 6)):
                spdma(stg[:, 0:PAYW], gsrc[j * 128:(j + 1) * 128, :], "stg",
                      reads=[("ccout",)] if fused else [], writes=[("stg",)])
                selj = parT[:, 160 + j:161 + j]
                act(ea, stg[:, 1024:1028], AF.Exp, [("stg",)], [("sm", "ea")])
                dve(lambda e, selj=selj: e.tensor_scalar(out=a3, in0=ea, scalar1=-1.0, scalar2=selj,
                                                          op0=ALU.add, op1=ALU.mult),
                    [("sm", "ea"), ("c", "par")], [("sm", "a3")])
                dve(lambda e: e.tensor_scalar(out=a3, in0=a3, scalar1=1.0, scalar2=None, op0=ALU.add),
                    [("sm", "a3")], [("sm", "a3")])
                for hd in range(4):
                    sj = stg[:, hd * 256:(hd + 1) * 256]
                    dve(lambda e, sj=sj, selj=selj: e.tensor_scalar(out=sj, in0=sj, scalar1=selj,
                                                                    scalar2=None, op0=ALU.mult),
                        [("stg",), ("c", "par")], [("stg",)])
                    dve(lambda e, hd=hd, sj=sj: e.scalar_tensor_tensor(
                        out=Sv[:, hd, :], in0=Sv[:, hd, :], scalar=a3[:, hd:hd + 1], in1=sj,
                        op0=ALU.mult, op1=ALU.add),
                        [("stg",), ("sm", "a3"), ("S", hd)], [("S", hd)])

            cur = 0
            act(S_bf[0], Sst[:, 0:1024], AF.Copy, [("S", hd) for hd in range(4)], [("Sbf", 0)])
            og0 = 144 + l * 8
            it = [0]
            for tt in range(NTT):
                th = tt // 4
                tsl = slice(tt * 128, (tt + 1) * 128)
                for hd in range(4):
                    i2 = it[0] % 2
                    it[0] += 1
                    psc = psb[i2][:, 0:128]
                    kpsc = ("ps", i2)
                    mm(psc, kd[:, hd, tsl], qd[:, hd, tsl], True, True,
                       [("kd", hd, th), ("qd", hd, th)], [kpsc])
                    sc = scm[i2]
                    ksc = ("scm", i2)
                    dve(lambda e, sc=sc, psc=psc: e.tensor_tensor(out=sc, in0=psc, in1=maskT[:], op=ALU.mult),
                        [kpsc, ("c", "mk")], [ksc])
                    po = psb[2 + i2][:, 0:256]
                    kpo = ("ps", 2 + i2)
                    for vc in range(2):
                        vs = slice(hd * 256 + vc * 128, hd * 256 + (vc + 1) * 128)
                        mm(po[:, vc * 128:(vc + 1) * 128], v_tok[:, tt, vs], sc, True, False,
                           [("vt", tt), ksc], [kpo])
                        mm(po[:, vc * 128:(vc + 1) * 128], S_bf[cur][:, vs], qd[:, hd, tsl], False, True,
                           [("Sbf", cur), ("qd", hd, th)], [kpo])
                    osq = p2b[i2]
                    kosq = ("osq", i2)
                    act(osq, po, AF.Square, [kpo], [kosq])
                    pss = psb[4 + i2][:, 0:128]
                    kpss = ("ps", 4 + i2)
                    mm(pss, ones_bf[:], osq[:, 0:128], True, False, [kosq, ("c", "ones")], [kpss])
                    mm(pss, ones_bf[:], osq[:, 128:256], False, True, [kosq], [kpss])
                    rs = p2f[i2]
                    krs = ("p2rs", i2)
                    act(rs, pss, AF.Ln, [kpss, ("c", "eps")], [krs], bias=epsT[:], scale=1.0 / 256.0)
                    act(rs, rs, AF.Exp, [krs], [krs], scale=-0.5)
                    for vc in range(2):
                        c = hd * 2 + vc
                        tq = p2f[2 + vc]
                        ktq = ("p2t", vc)
                        dve(lambda e, tq=tq, po=po, vc=vc, c=c, rs=rs: e.scalar_tensor_tensor(
                            out=tq, in0=po[:, vc * 128:(vc + 1) * 128],
                            scalar=parT[:, og0 + c:og0 + c + 1], in1=rs, op0=ALU.mult, op1=ALU.mult),
                            [kpo, krs, ("c", "par")], [ktq])
                        ys = r_s[:, c, tsl]
                        dve(lambda e, ys=ys, tq=tq: e.tensor_tensor(out=ys, in0=tq, in1=ys, op=ALU.mult),
                            [ktq, ("rs", c, th)], [("rs", c, th)])
                if tt < NTT - 1 or (seq and cur_seg[0] < NSEG - 1):
                    for hd in range(4):
                        bank = 6 + (hd % 2)
                        pS = psb[bank][:, 0:256]
                        pk = ("ps", bank)
                        mm(pS, kd_tok[:, tt, hd * 128:(hd + 1) * 128], v_tok[:, tt, hd * 256:(hd + 1) * 256],
                           True, True, [("kdt", tt), ("vt", tt)], [pk])
                        dve(lambda e, hd=hd, pS=pS, tt=tt: e.scalar_tensor_tensor(
                            out=Sv[:, hd, :], in0=Sv[:, hd, :], scalar=dec[:, hd, tt:tt + 1], in1=pS,
                            op0=ALU.mult, op1=ALU.add),
                            [pk, ("S", hd), ("sm", "dec")], [("S", hd)])
                    if tt < NTT - 1:
                        act(S_bf[1 - cur], Sst[:, 0:1024], AF.Copy, [("S", hd) for hd in range(4)],
                            [("Sbf", 1 - cur)])
                        cur = 1 - cur
            if seq and cur_seg[0] < NSEG - 1:
                spdma(gla_state[l], Sst[:, 0:1024], "gst", reads=[("S", hd) for hd in range(4)],
                      writes=[("gst", l)])
            mix_out_half(l, 1024, r_s, "rs")
            P.muted = False
            return False

        def ple_stage(l):
            P.guard([("pTb",)])
            sg0 = cur_seg[0] * NT
            d_pT = wl("pT", l).rearrange("(k p) t -> p k t", p=128)[:, :, sg0:sg0 + NT]
            P.emit("pool", lambda e: e.dma_start(out=pTb[:], in_=d_pT), writes=[("pTb",)], dma="pin")
            si = next_aslot()
            wp = aslot(si, "p (k c) -> p k c", k=2)
            wpk = ("A", si)
            wdma(wp[:], wl("w_ple_proj", l).rearrange("(k p) c -> p k c", p=128), [wpk])
            wv = wl("w_ple_gate", l).rearrange("(k p) c -> p k c", p=128)

            def evac(fc, banks):
                pb = (4, 5) if fc % 2 == 0 else (6, 7)
                for k2 in range(2):
                    for th in range(TH):
                        mm(psb[pb[th]][:], wp[:, k2, fc * 128:(fc + 1) * 128], pTb[:, k2, th * 512:(th + 1) * 512],
                           k2 == 0, k2 == 1, [wpk, ("pTb",)], [("ps", pb[th])])
                for th in range(TH):
                    i = (fc * 2 + th) % 4
                    t = ftmp[i]
                    tk = ("ftmp", i)
                    act(t, psb[banks[th]][:], AF.Sigmoid, [("ps", banks[th])], [tk])
                    dve(lambda e, t=t, ps=psb[pb[th]][:]: e.tensor_tensor(out=t, in0=ps, in1=t, op=ALU.mult),
                        [("ps", pb[th]), tk], [tk])
                    hs = h[:, fc, th * 512:(th + 1) * 512]
                    dve(lambda e, hs=hs, t=t: e.tensor_tensor(out=hs, in0=hs, in1=t, op=ALU.add),
                        [tk, hk(fc, th)], [hk(fc, th)])

            astate["keep"] = si
            proj_fm(wv, 0, KC, evac)
            astate["keep"] = None

        ffn_keys = [("actb", s, fc, th) for s in range(2) for fc in range(GC) for th in range(TH)] + \
                   [("ftmp", i) for i in range(4)]

        def save_state():
            toks = P.snapshot()
            hv = dout("h_out", [128, KC * NT])
            for k in range(KC):
                spdma(hv[:, k * NT:(k + 1) * NT], h[:, k, :], "out", extra=toks)
            spdma(dout("nb_out", [128, KC * NT], BF16)[:, :], nb[:].rearrange("p k t -> p (k t)"), "out",
                  extra=toks)
            spdma(dout("r1_out", [128, 20480], BF16)[:, :], R1[:], "out", extra=toks)
            spdma(dout("sm_out", [128, 256])[:, :], small[:], "out", extra=toks)

        def restore_state():
            hv = din("h_sv", [128, KC * NT])
            for k in range(KC):
                spdma(h[:, k, :], hv[:, k * NT:(k + 1) * NT], "rst")
            spdma(nb[:].rearrange("p k t -> p (k t)"), din("nb_sv", [128, KC * NT], BF16)[:, :], "rst")
            spdma(R1[:], din("r1_sv", [128, 20480], BF16)[:, :], "rst")
            spdma(small[:], din("sm_sv", [128, 256])[:, :], "rst")
            total = P.dcnt["rst"]
            for e in P.ops:
                P.ops[e].append(([("rst", total)], None, None))
                P.waited[e]["rst"] = total

        def layer_a(l, part):
            P.guard(ffn_keys + [("B", 0), ("B", 1)])
            norm_stage((l * 4 + 0) * 16)
            ffn_stage(wl("w_ffn1_in", l), wl("w_ffn1_out", l))
            norm_stage((l * 4 + 1) * 16)
            mixer_stage(l, part)

        def layer_b(l):
            P.guard(ffn_keys + [("B", 0), ("B", 1)])
            norm_stage((l * 4 + 2) * 16)
            ffn_stage(wl("w_ffn2_in", l), wl("w_ffn2_out", l))
            norm_stage((l * 4 + 3) * 16)
            ple_stage(l)

        def store_out(seg=0):
            ov = dout("outT", [D, SEQT]).rearrange("(k p) t -> p k t", p=128)
            for k in range(KC):
                spdma(ov[:, k, seg * NT:(seg + 1) * NT], h[:, k, :], "out", reads=[hk(k, 0), hk(k, 1)])

        if seq:
            for seg in range(NSEG):
                cur_seg[0] = seg
                load_x(seg)
                for l in range(DEPTH):
                    layer_a(l, 0)
                    layer_b(l)
                norm_stage(8 * 16, final=True)
                store_out(seg)
        elif fused:
            load_x()
            for l in range(DEPTH):
                layer_a(l, 0)
                layer_b(l)
            norm_stage(8 * 16, final=True)
            store_out()
        elif phase == 0:
            load_x()
            layer_a(0, 1)
            save_state()
        elif phase == 1:
            restore_state()
            mixer_stage(0, 2)
            layer_b(0)
            layer_a(1, 1)
            save_state()
        else:
            restore_state()
            mixer_stage(1, 2)
            layer_b(1)
            norm_stage(8 * 16, final=True)
            store_out()
        P.ops["sp"].append(([("out", P.dcnt["out"])], None, None))

        def runner(name):
            def body(eng):
                for waits, fn, inc in P.ops[name]:
                    for (k, v) in waits:
                        eng.wait_ge(sems[k], v)
                    if fn is None:
                        continue
                    ins = fn(eng)
                    if inc is not None:
                        ins.then_inc(sems[inc[0]], inc[1])
            return body

        block.sync(runner("sp"))
        block.gpsimd(runner("pool"))
        block.scalar(runner("act"))
        block.vector(runner("dve"))
        block.tensor(runner("pe"))
    return nc, list(needed.keys()), list(outs_decl.keys())


_CACHE = {}
MODE = "seq"


def _make_provider(inputs):
    x = np.asarray(inputs["x"], dtype=np.float32).reshape(-1, D)
    p = np.asarray(inputs["p"], dtype=np.float32).reshape(DEPTH, -1, 256)
    gains = []
    for l in range(DEPTH):
        for n in ("ffn1_norm", "mix_norm", "ffn2_norm", "ple_norm"):
            gains.append(np.asarray(inputs[n], dtype=np.float32)[l].reshape(KC, 128).T)
    gains.append(np.asarray(inputs["final_norm"], dtype=np.float32).reshape(KC, 128).T)
    og = [np.asarray(inputs["gla_o_gain"], dtype=np.float32)[l].reshape(8, 128).T for l in range(DEPTH)]
    shared = {}

    def get(name, c, extra):
        if name in extra:
            return extra[name][c]
        if name == "xT":
            return np.ascontiguousarray(x[c * NT:(c + 1) * NT].T)
        if name == "par":
            sel = np.zeros((128, 8), np.float32)
            for j in range(NCORES):
                if j // 4 == c // 4 and j < c:
                    sel[:, j] = 1.0
            return np.ascontiguousarray(np.concatenate(gains + og + [sel], axis=1).astype(np.float32))
        base, l = name.rsplit("_", 1)
        l = int(l)
        if base == "pT":
            return np.ascontiguousarray(p[l, c * NT:(c + 1) * NT, :].T)
        if name not in shared:
            if base == "sg_wT":
                a = np.asarray(inputs["sg_w"], dtype=np.float32)[l].transpose(2, 0, 1).reshape(128, 1024)
            elif base in ("sg_b", "sg_v_gain", "gla_b_gate"):
                a = np.asarray(inputs[base], dtype=np.float32)[l].reshape(1, -1)
            else:
                a = np.asarray(inputs[base], dtype=np.float32)[l]
            shared[name] = np.ascontiguousarray(a)
        return shared[name]

    return get


def kernel(**inputs):
    get = _make_provider(inputs)
    cores = list(range(NCORES))
    if MODE == "seq":
        if "seq" not in _CACHE:
            _CACHE["seq"] = build_program(seq=True)
        nc, needed, _ = _CACHE["seq"]
        x = np.asarray(inputs["x"], dtype=np.float32)
        p = np.asarray(inputs["p"], dtype=np.float32)
        in_maps = []
        for b in range(2):
            m = {}
            for n in needed:
                if n == "xT":
                    m[n] = np.ascontiguousarray(x[b].T)
                elif n.startswith("pT_"):
                    m[n] = np.ascontiguousarray(p[int(n[3:]), b].T)
                else:
                    m[n] = get(n, 0, {})
            in_maps.append(m)
        res = run_bass_kernel_spmd(nc, in_maps, core_ids=[0, 1])
        out = np.stack([np.asarray(r["outT"]).T for r in res.results], axis=0)
        return np.ascontiguousarray(out.astype(np.float32))
    if MODE == "fused":
        if "fused" not in _CACHE:
            _CACHE["fused"] = build_program()
        nc, needed, _ = _CACHE["fused"]
        in_maps = [{n: get(n, c, {}) for n in needed} for c in cores]
        res = run_bass_kernel_spmd(nc, in_maps, core_ids=cores)
    else:
        extra = {}
        for ph in range(3):
            key = ("phase", ph)
            if key not in _CACHE:
                _CACHE[key] = build_program(phase=ph)
            nc, needed, _ = _CACHE[key]
            in_maps = [{n: get(n, c, extra) for n in needed} for c in cores]
            res = run_bass_kernel_spmd(nc, in_maps, core_ids=cores)
            if ph < 2:
                extra = {k + "_sv": [np.asarray(res.results[c][k + "_out"]) for c in cores]
                         for k in ("h", "nb", "r1", "sm")}
                gath = np.ascontiguousarray(
                    np.concatenate([np.asarray(res.results[c]["payload"]) for c in cores], axis=0))
                extra["gath"] = [gath] * NCORES
    outs = [np.asarray(r["outT"]).T for r in res.results]
    out = np.concatenate(outs, axis=0).reshape(2, 4096, D).astype(np.float32)
    return out
```

```python
import numpy as np
import concourse.bass as bass
import concourse.mybir as mybir
from concourse.bass_utils import run_bass_kernel_spmd

F32 = mybir.dt.float32
BF16 = mybir.dt.bfloat16
AF = mybir.ActivationFunctionType
ALU = mybir.AluOpType
AX = mybir.AxisListType

NCORES = 8
D = 2048
KC = 16
NT = 1024
TH = 2
NTT = 8
DFF = 5632
NFC = 44
GC = 4
NG = 11
DEPTH = 2
INW = 5136
EPS = 1e-6
NPAR = 9 * 16 + 16 + 8
PAYW = 1028


class Plan:
    CE = ("pe", "act", "dve", "pool")

    def __init__(self):
        self.ops = {e: [] for e in ("pe", "act", "dve", "pool", "sp")}
        self.cnt = {e: 0 for e in self.CE}
        self.pend = {e: False for e in self.CE}
        self.waited = {e: {} for e in self.ops}
        self.res = {}
        self.dcnt = {}
        self.muted = False

    def _res(self, k):
        r = self.res.get(k)
        if r is None:
            r = self.res[k] = [None, {}]
        return r

    def snapshot(self):
        return [(e, self.cnt[e] + (1 if self.pend[e] else 0)) for e in self.CE
                if self.cnt[e] or self.pend[e]]

    def guard(self, keys, toks=None):
        if self.muted:
            return
        if toks is None:
            toks = self.snapshot()
        for k in keys:
            R = self._res(k)
            for (s, v) in toks:
                if R[1].get(s, 0) < v:
                    R[1][s] = v

    def emit(self, e, fn, reads=(), writes=(), extra=(), signal=True, dma=None, dma_inc=16):
        if self.muted:
            return None
        deps = {}
        own_raw = 0

        def add(tok, raw=False):
            nonlocal own_raw
            if tok is None:
                return
            k, v = tok
            if k == e:
                if raw and v > own_raw:
                    own_raw = v
                return
            if v > deps.get(k, 0):
                deps[k] = v

        for r in reads:
            R = self._res(r)
            add(R[0], True)
        for w in writes:
            R = self._res(w)
            add(R[0])
            for k, v in R[1].items():
                add((k, v))
        for t in extra:
            add(t, True)
        wd = self.waited[e]
        waits = []
        for k, v in deps.items():
            if wd.get(k, 0) < v:
                wd[k] = v
                waits.append((k, v))
        if own_raw and e in ("act", "dve", "pool") and wd.get(e, 0) < own_raw:
            wd[e] = own_raw
            waits.append((e, own_raw))
        if dma is not None:
            self.dcnt[dma] = self.dcnt.get(dma, 0) + dma_inc
            tok = (dma, self.dcnt[dma])
            inc = (dma, dma_inc)
        elif signal:
            self.cnt[e] += 1
            self.pend[e] = False
            tok = (e, self.cnt[e])
            inc = (e, 1)
        else:
            self.pend[e] = True
            tok = (e, self.cnt[e] + 1)
            inc = None
        k, v = tok
        for r in reads:
            R = self._res(r)
            if R[1].get(k, 0) < v:
                R[1][k] = v
        for w in writes:
            R = self._res(w)
            R[0] = tok
            R[1] = {}
        self.ops[e].append((waits, fn, inc))
        return tok


def build_program(stop_after=None, phase=None, seq=False):
    nc = bass.Bass("TRN2", target_bir_lowering=False)
    P = Plan()
    dt = nc.dram_tensor
    needed = {}
    outs_decl = {}

    dummies = {}

    def din(name, shape, dtype=F32):
        if name in needed:
            return needed[name]
        if P.muted:
            if name not in dummies:
                dummies[name] = dt("dmy_" + name, list(shape), dtype).ap()
            return dummies[name]
        needed[name] = dt(name, list(shape), dtype, kind="ExternalInput").ap()
        return needed[name]

    def dout(name, shape, dtype=F32):
        if name not in outs_decl:
            outs_decl[name] = dt(name, list(shape), dtype, kind="ExternalOutput").ap()
        return outs_decl[name]

    LSHAPES = {
        "w_ffn1_in": [D, 2 * DFF], "w_ffn1_out": [DFF, D], "w_mix_in": [D, INW], "w_mix_out": [D, D],
        "w_ffn2_in": [D, 2 * DFF], "w_ffn2_out": [DFF, D], "w_ple_gate": [D, D], "w_ple_proj": [256, D],
        "sg_v_gain": [1, 1024], "sg_wT": [128, 1024], "sg_b": [1, 1024], "gla_w_gate": [16, 512],
        "gla_b_gate": [1, 512], "pT": [256, NT],
    }

    def wl(name, l):
        return din(f"{name}_{l}", LSHAPES[name])

    NSEG = 4 if seq else 1
    SEQT = NT * NSEG
    if seq:
        LSHAPES["pT"] = [256, SEQT]
        gla_state = dt("gla_state", [DEPTH, 128, 1024], F32)
    cur_seg = [0]
    fused = (phase is None) and not seq
    if fused:
        cc_in = dt("cc_in", [128, PAYW], F32)
        cc_out = dt("cc_out", [NCORES * 128, PAYW], F32)

    sb = nc.sbuf_tensor
    import contextlib
    es = contextlib.ExitStack()
    with es:
        ent = es.enter_context
        h = ent(sb("h", [128, KC, NT], F32))
        nb = ent(sb("nb", [128, KC, NT], BF16))
        aslots = ent(sb("aslots", [128, 3, KC * 256], BF16))
        R2 = ent(sb("R2", [128, 16384], BF16))
        R1 = ent(sb("R1", [128, 20480], BF16))
        ones_bf = ent(sb("ones_bf", [128, 128], BF16))
        U_inc = ent(sb("U_inc", [128, 128], BF16))
        U_str = ent(sb("U_str", [128, 128], BF16))
        maskT = ent(sb("maskT", [128, 128], F32))
        epsT = ent(sb("epsT", [128, 1], F32))
        parT = ent(sb("parT", [128, NPAR], F32))
        wTm = ent(sb("wTm", [128, 8, 128], BF16))
        bs_row = ent(sb("bs_row", [1, 1024], BF16))
        wgate = ent(sb("wgate", [17, 512], BF16))
        vgain = ent(sb("vgain", [128, 1024], F32))
        small = ent(sb("small", [128, 256], F32))
        psb = [ent(nc.psum_tensor(f"ps{i}", [128, 512], F32)) for i in range(8)]
        sem_names = ["pe", "act", "dve", "pool", "xin", "par", "lwT", "lbs", "lwg", "lvg", "pin", "A0", "A1", "A2",
                     "B0", "B1", "cci", "cc", "stg", "out", "rst", "gst"]
        sems = {n: ent(nc.semaphore(n)) for n in sem_names}
        block = ent(nc.Block())

        def carve(reg, off, nbytes, dtype=BF16, pat=None, **kw):
            v = reg[:, off // 2:(off + nbytes) // 2]
            if dtype == F32:
                v = v.bitcast(F32)
            if pat:
                v = v.rearrange(pat, **kw)
            return v

        KB = 1024
        actb = [carve(R1, i * 8 * KB, 8 * KB, BF16, "p (c t) -> p c t", c=GC) for i in range(2)]
        ftmp = [carve(R1, 16 * KB + i * 2 * KB, 2 * KB, F32) for i in range(4)]
        wout = [carve(R2, i * 16 * KB, 16 * KB, BF16, "p (c d) -> p c d", c=GC) for i in range(2)]
        u_g = carve(R1, 0, 16 * KB, BF16, "p (c t) -> p c t", c=8)
        v_ln = carve(R1, 16 * KB, 16 * KB, BF16, "p (t c) -> p t c", t=NTT)
        vgt = [carve(R1, 32 * KB + i * KB, KB, F32) for i in range(4)]
        sqt = [carve(R1, 36 * KB + i * KB, KB, F32) for i in range(2)]
        qd = carve(R1, 0, 8 * KB, BF16, "p (c t) -> p c t", c=4)
        kd = carve(R1, 8 * KB, 8 * KB, BF16, "p (c t) -> p c t", c=4)
        kd_tok = carve(R1, 16 * KB, 8 * KB, BF16, "p (t c) -> p t c", t=NTT)
        v_tok = carve(R1, 24 * KB, 16 * KB, BF16, "p (t c) -> p t c", t=NTT)
        sp_hi = carve(R2, 0, 8 * KB, BF16, "p (t c) -> p t c", t=NTT)
        sp_lo = carve(R2, 8 * KB, 8 * KB, BF16, "p (t c) -> p t c", t=NTT)
        zT = carve(R2, 16 * KB, 2 * KB, BF16)
        mtmp = [carve(R2, 18 * KB + i * 2 * KB, 2 * KB, F32) for i in range(3)]
        r_s = carve(R2, 0, 16 * KB, BF16, "p (c t) -> p c t", c=8)
        S_bf = [carve(R2, 16 * KB + i * 2 * KB, 2 * KB, BF16) for i in range(2)]
        p2f = [carve(R2, 20 * KB + i * 512, 512, F32) for i in range(4)]
        p2b = [carve(R2, 22 * KB + i * 512, 512, BF16) for i in range(2)]
        scm = [carve(R2, 23 * KB + i * 256, 256, BF16) for i in range(2)]
        stg = carve(R2, 23 * KB + 512, 4128, F32)
        Sst = carve(R2, 28 * KB - 32, 4128, F32)
        pTb = carve(R1, 24 * KB, 4 * KB, BF16, "p (k t) -> p k t", k=2)

        def aslot(i, pat=None, **kw):
            v = aslots[:, i, :]
            if pat:
                v = v.rearrange(pat, **kw)
            return v

        blast = small[:, 32:64].rearrange("p (h t) -> p h t", h=4)
        dec = small[:, 64:96].rearrange("p (h t) -> p h t", h=4)
        ea = small[:, 96:100]
        a3 = small[:, 100:104]

        astate = {"i": 0, "keep": None}

        def next_aslot():
            while True:
                i = astate["i"] % 3
                astate["i"] += 1
                if i != astate["keep"]:
                    return i

        def mm(out, lhsT, rhs, start, stop, reads, writes, extra=(), sig=False):
            return P.emit("pe", lambda e: e.matmul(out, lhsT=lhsT, rhs=rhs, start=start, stop=stop),
                          reads=reads, writes=writes, extra=extra, signal=bool(stop) or sig)

        def act(out, in_, func, reads, writes, bias=None, scale=None):
            kw = {}
            if bias is not None:
                kw["bias"] = bias
            if scale is not None:
                kw["scale"] = scale
            return P.emit("act", lambda e: e.activation(out=out, in_=in_, func=func, **kw),
                          reads=reads, writes=writes)

        def dve(fn, reads, writes):
            return P.emit("dve", fn, reads=reads, writes=writes)

        def wdma(out, in_, writes):
            slotkey = writes[0]
            semn = slotkey[0] + str(slotkey[1])
            return P.emit("pool", lambda e: e.dma_start(out=out, in_=in_), writes=writes, dma=semn)

        def spdma(out, in_, sem, reads=(), writes=(), extra=()):
            return P.emit("sp", lambda e: e.dma_start(out=out, in_=in_), reads=reads, writes=writes,
                          extra=extra, dma=sem)

        def hk(k, th):
            return ("h", k, th)

        def nbk(k, th):
            return ("nb", k, th)

        def pool_op(fn, writes, reads=()):
            return P.emit("pool", fn, reads=reads, writes=writes)

        pool_op(lambda e: e.memset(small[:], 0.0), [("sm", "blast"), ("sm", "dec"), ("sm", "ea"), ("sm", "a3")])
        pool_op(lambda e: e.memset(ones_bf[:], 1.0), [("c", "ones")])
        pool_op(lambda e: e.memset(epsT[:], EPS), [("c", "eps")])
        pool_op(lambda e: e.memset(U_inc[:], -1.0 / 16.0), [("c", "Ui")])
        pool_op(lambda e: e.affine_select(out=U_inc[:], in_=U_inc[:], pattern=[[1, 128]],
                                          compare_op=ALU.is_ge, fill=0.0, base=0, channel_multiplier=-1),
                [("c", "Ui")], [("c", "Ui")])
        pool_op(lambda e: e.memset(U_str[:], -1.0 / 16.0), [("c", "Us")])
        pool_op(lambda e: e.affine_select(out=U_str[:], in_=U_str[:], pattern=[[-1, 128]],
                                          compare_op=ALU.is_gt, fill=0.0, base=0, channel_multiplier=1),
                [("c", "Us")], [("c", "Us")])
        pool_op(lambda e: e.memset(maskT[:], 1.0), [("c", "mk")])
        pool_op(lambda e: e.affine_select(out=maskT[:], in_=maskT[:], pattern=[[1, 128]],
                                          compare_op=ALU.is_ge, fill=0.0, base=0, channel_multiplier=-1),
                [("c", "mk")], [("c", "mk")])
        CONST = [("c", "ones"), ("c", "eps"), ("c", "Ui"), ("c", "Us"), ("c", "mk"), ("c", "par")]

        spdma(parT[:], din("par", [128, NPAR])[:, :], "par", writes=[("c", "par")])
        def load_x(seg=0):
            xv = din("xT", [D, SEQT]).rearrange("(k p) t -> p k t", p=128)
            for k in range(KC):
                spdma(h[:, k, :], xv[:, k, seg * NT:(seg + 1) * NT], "xin", writes=[hk(k, 0), hk(k, 1)])
            for k in range(KC):
                for th in range(TH):
                    P.res[hk(k, th)][0] = ("xin", 16 * KC * (seg + 1))

        def norm_stage(gcol, final=False):
            for th in range(TH):
                ts = slice(th * 512, (th + 1) * 512)
                bank = 6 + th
                for k in range(KC):
                    t = ftmp[k % 2].bitcast(BF16)[:, 0:512]
                    tk = ("ftmp", k % 2)
                    act(t, h[:, k, ts], AF.Square, [hk(k, th)], [tk])
                    mm(psb[bank][:], ones_bf[:], t, k == 0, k == KC - 1,
                       [tk, ("c", "ones")], [("ps", bank)], sig=True)
                rt = ftmp[2 + th]
                rk = ("ftmp", 2 + th)
                act(rt, psb[bank][:], AF.Ln, [("ps", bank), ("c", "eps")], [rk], bias=epsT[:], scale=1.0 / D)
                act(rt, rt, AF.Exp, [rk], [rk], scale=-0.5)
                for k in range(KC):
                    g = parT[:, gcol + k:gcol + k + 1]
                    if final:
                        o = h[:, k, ts]
                        wr = [hk(k, th)]
                    else:
                        o = nb[:, k, ts]
                        wr = [nbk(k, th)]
                    dve(lambda e, o=o, i0=h[:, k, ts], g=g, rt=rt: e.scalar_tensor_tensor(
                        out=o, in0=i0, scalar=g, in1=rt, op0=ALU.mult, op1=ALU.mult),
                        [hk(k, th), rk, ("c", "par")], wr)

        def ffn_stage(w_in, w_out):
            wiv = w_in.rearrange("(k p) c -> p k c", p=128)
            wov = w_out.rearrange("(c p) d -> p c d", p=128)
            ti = [0]

            def p1(g):
                ab = actb[g % 2]
                for fc in range(GC):
                    f = g * GC + fc
                    si = next_aslot()
                    sl = aslot(si, "p (k c) -> p k c", k=KC)
                    sk = ("A", si)
                    wdma(sl[:, :, 0:128], wiv[:, :, f * 128:(f + 1) * 128], [sk])
                    wdma(sl[:, :, 128:256], wiv[:, :, DFF + f * 128:DFF + (f + 1) * 128], [sk])
                    for half, banks in ((0, (0, 1)), (1, (2, 3))):
                        for k in range(KC):
                            for th in range(TH):
                                mm(psb[banks[th]][:], sl[:, k, half * 128:(half + 1) * 128],
                                   nb[:, k, th * 512:(th + 1) * 512], k == 0, k == KC - 1,
                                   [sk, nbk(k, th)], [("ps", banks[th])])
                    for th in range(TH):
                        t = ftmp[ti[0] % 4]
                        tk = ("ftmp", ti[0] % 4)
                        ti[0] += 1
                        act(t, psb[th][:], AF.Silu, [("ps", th)], [tk])
                        dve(lambda e, o=ab[:, fc, th * 512:(th + 1) * 512], i0=psb[2 + th][:], t=t:
                            e.tensor_tensor(out=o, in0=i0, in1=t, op=ALU.mult),
                            [("ps", 2 + th), tk], [("actb", g % 2, fc, th)])

            def p2(g):
                ab = actb[g % 2]
                bi = g % 2
                bk = ("B", bi)
                for j in range(KC):
                    banks = (4, 5) if j % 2 == 0 else (6, 7)
                    for fc in range(GC):
                        for th in range(TH):
                            mm(psb[banks[th]][:], wout[bi][:, fc, j * 128:(j + 1) * 128],
                               ab[:, fc, th * 512:(th + 1) * 512], fc == 0, fc == GC - 1,
                               [bk, ("actb", g % 2, fc, th)], [("ps", banks[th])])
                    for th in range(TH):
                        hs = h[:, j, th * 512:(th + 1) * 512]
                        dve(lambda e, hs=hs, ps=psb[banks[th]][:]: e.scalar_tensor_tensor(
                            out=hs, in0=ps, scalar=0.5, in1=hs, op0=ALU.mult, op1=ALU.add),
                            [("ps", banks[th]), hk(j, th)], [hk(j, th)])

            def loadB(g):
                wdma(wout[g % 2][:], wov[:, g * GC:(g + 1) * GC, :], [("B", g % 2)])

            p1(0)
            loadB(0)
            for g in range(1, NG):
                p1(g)
                p2(g - 1)
                if g < NG:
                    loadB(g)
            p2(NG - 1)

        def proj_fm(wv, c0, nchunks, evac, nk=KC, rhs_fn=None, rkey_fn=None):
            if rhs_fn is None:
                rhs_fn = lambda k, th: nb[:, k, th * 512:(th + 1) * 512]
                rkey_fn = nbk
            cpp = (KC * 256) // (nk * 128)
            for s0 in range(0, nchunks, cpp):
                n = min(cpp, nchunks - s0)
                si = next_aslot()
                sl = aslot(si, "p (k c) -> p k c", k=nk)
                sk = ("A", si)
                wdma(sl[:, :, 0:n * 128], wv[:, :, c0 + s0 * 128:c0 + (s0 + n) * 128], [sk])
                for j in range(n):
                    fc = s0 + j
                    banks = (0, 1) if fc % 2 == 0 else (2, 3)
                    for k in range(nk):
                        for th in range(TH):
                            mm(psb[banks[th]][:], sl[:, k, j * 128:(j + 1) * 128], rhs_fn(k, th),
                               k == 0, k == nk - 1, [sk, rkey_fn(k, th)], [("ps", banks[th])])
                    evac(fc, banks)

        def proj_tok(wv, c0, nslots, evac):
            for q in range(nslots):
                si = next_aslot()
                sl = aslot(si, "p (k c) -> p k c", k=KC)
                sk = ("A", si)
                wdma(sl[:, :, :], wv[:, :, c0 + q * 256:c0 + (q + 1) * 256], [sk])
                for tt in range(NTT):
                    bank = 4 + (tt % 2)
                    pk = ("ps", bank)
                    o = psb[bank][:, 0:256]
                    th = tt // 4
                    for k in range(KC):
                        mm(o, nb[:, k, tt * 128:(tt + 1) * 128], sl[:, k, :], k == 0, k == KC - 1,
                           [sk, nbk(k, th)], [pk])
                    evac(q, tt, o, pk)

        def mix_out_half(l, r0, ybuf, ykey):
            wv = wl("w_mix_out", l)[r0:r0 + 1024, :].rearrange("(k p) c -> p k c", p=128)

            def evac(fc, banks):
                for th in range(TH):
                    hs = h[:, fc, th * 512:(th + 1) * 512]
                    dve(lambda e, hs=hs, ps=psb[banks[th]][:]: e.tensor_tensor(
                        out=hs, in0=ps, in1=hs, op=ALU.add),
                        [("ps", banks[th]), hk(fc, th)], [hk(fc, th)])

            proj_fm(wv, 0, KC, evac, nk=8,
                    rhs_fn=lambda k, th: ybuf[:, k, th * 512:(th + 1) * 512],
                    rkey_fn=lambda k, th: (ykey, k, th))

        def mark(name):
            if stop_after == name:
                P.muted = True

        def mixer_stage(l, part=0):
            P.muted = (part == 2)
            wv = wl("w_mix_in", l).rearrange("(k p) c -> p k c", p=128)
            d_wT, d_sb, d_wg, d_bg = wl("sg_wT", l), wl("sg_b", l), wl("gla_w_gate", l), wl("gla_b_gate", l)
            P.emit("pool", lambda e: e.dma_start(out=wTm[:].rearrange("p h t -> p (h t)"), in_=d_wT),
                   writes=[("c", "wTm")], dma="lwT")
            P.emit("pool", lambda e: e.dma_start(out=bs_row[:], in_=d_sb),
                   writes=[("c", "bs")], dma="lbs")
            P.emit("pool", lambda e: e.dma_start(out=wgate[0:16, :], in_=d_wg),
                   writes=[("c", "wg")], dma="lwg")
            P.emit("pool", lambda e: e.dma_start(out=wgate[16:17, :], in_=d_bg),
                   writes=[("c", "wg")], dma="lwg")
            P.emit("pool", lambda e: e.affine_select(out=wTm[:], in_=wTm[:], pattern=[[0, 8], [1, 128]],
                                                     compare_op=ALU.is_ge, fill=0.0, base=0,
                                                     channel_multiplier=-1),
                   reads=[("c", "wTm")], writes=[("c", "wTm")])
            spdma(vgain[:], wl("sg_v_gain", l).broadcast_to([128, 1024]), "lvg", writes=[("c", "vg")])

            sgkeys = [("ug", c, th) for c in range(8) for th in range(TH)] + \
                     [("vln", tt) for tt in range(NTT)] + [("vgt", i) for i in range(4)] + \
                     [("sqt", i) for i in range(2)]
            P.guard(sgkeys)

            def evac_u(fc, banks):
                for th in range(TH):
                    act(u_g[:, fc, th * 512:(th + 1) * 512], psb[banks[th]][:], AF.Gelu,
                        [("ps", banks[th])], [("ug", fc, th)])

            proj_fm(wv, 0, 8, evac_u)

            vi = [0]

            def evac_vsg(q, tt, ps, pk):
                i = vi[0] % 4
                vi[0] += 1
                vg = vgt[i]
                vk = ("vgt", i)
                sm0 = 128 + i * 20
                sm_stat = small[:, sm0:sm0 + 12]
                sm_mv = small[:, sm0 + 12:sm0 + 16]
                sm_rs = small[:, sm0 + 16:sm0 + 18]
                sm_ln = small[:, sm0 + 18:sm0 + 20]
                kst, kmv, krs_, kln = ("smst", i), ("smmv", i), ("smrs", i), ("smln", i)
                act(vg, ps, AF.Gelu, [pk], [vk])
                for hh in range(2):
                    dve(lambda e, hh=hh, vg=vg, sm_stat=sm_stat: e.bn_stats(
                        out=sm_stat[:, hh * 6:(hh + 1) * 6], in_=vg[:, hh * 128:(hh + 1) * 128]),
                        [vk], [kst])
                    dve(lambda e, hh=hh, sm_stat=sm_stat, sm_mv=sm_mv: e.bn_aggr(
                        out=sm_mv[:, hh * 2:(hh + 1) * 2], in_=sm_stat[:, hh * 6:(hh + 1) * 6]),
                        [kst], [kmv])
                mvv = sm_mv.rearrange("p (h c) -> p h c", c=2)
                act(sm_ln.unsqueeze(2), mvv[:, :, 1:2], AF.Ln, [kmv, ("c", "eps")], [kln],
                    bias=epsT[:], scale=1.0)
                act(sm_rs, sm_ln, AF.Exp, [kln], [krs_], scale=-0.5)
                for hh in range(2):
                    dve(lambda e, hh=hh, vg=vg, sm_mv=sm_mv, sm_rs=sm_rs: e.tensor_scalar(
                        out=vg[:, hh * 128:(hh + 1) * 128], in0=vg[:, hh * 128:(hh + 1) * 128],
                        scalar1=sm_mv[:, hh * 2:hh * 2 + 1], scalar2=sm_rs[:, hh:hh + 1],
                        op0=ALU.subtract, op1=ALU.mult),
                        [vk, kmv, krs_], [vk])
                dve(lambda e, vg=vg, q=q, tt=tt: e.tensor_tensor(
                    out=v_ln[:, tt, q * 256:(q + 1) * 256], in0=vg, in1=vgain[:, q * 256:(q + 1) * 256],
                    op=ALU.mult),
                    [vk, ("c", "vg")], [("vln", tt)])

            proj_tok(wv, 1024, 4, evac_vsg)

            mi = [0]
            for tt in range(NTT):
                th = tt // 4
                for hd in range(8):
                    r = mi[0] % 4
                    mi[0] += 1
                    pm = psb[4 + r][:, 0:128]
                    pk = ("ps", 4 + r)
                    mm(pm, v_ln[:, tt, hd * 128:(hd + 1) * 128], wTm[:, hd, :], True, False,
                       [("vln", tt), ("c", "wTm")], [pk])
                    mm(pm, ones_bf[0:1, :], bs_row[0:1, hd * 128:(hd + 1) * 128], False, True,
                       [("c", "ones"), ("c", "bs")], [pk])
                    us = u_g[:, hd, tt * 128:(tt + 1) * 128]
                    dve(lambda e, us=us, pm=pm: e.tensor_tensor(out=us, in0=pm, in1=us, op=ALU.mult),
                        [pk, ("ug", hd, th)], [("ug", hd, th)])
            mix_out_half(l, 0, u_g, "ug")
            if stop_after == ("mixA", l):
                return True

            mark("sg")
            m1keys = [("qd", c, th) for c in range(4) for th in range(TH)] + \
                     [("kd", c, th) for c in range(4) for th in range(TH)] + \
                     [("kdt", tt) for tt in range(NTT)] + [("vt", tt) for tt in range(NTT)] + \
                     [("sp", tt) for tt in range(NTT)] + [("spl", tt) for tt in range(NTT)] + [("zT",), ("mtmp", 0), ("mtmp", 1), ("mtmp2", 0),
                                                          ("mtmp2", 1), ("S", 0), ("S", 1), ("S", 2),
                                                          ("S", 3), ("S", "B"), ("stg",)]
            P.guard(m1keys)
            dve(lambda e: e.memset(zT[0:32, :], 1.0), [], [("zT",)])
            si = next_aslot()
            slz = aslot(si, "p (k c) -> p k c", k=KC)
            skz = ("A", si)
            wdma(slz[:, :, 0:16], wv[:, :, 5120:5136], [skz])
            for k in range(KC):
                for th in range(TH):
                    mm(psb[th][0:16, :], slz[:, k, 0:16], nb[:, k, th * 512:(th + 1) * 512],
                       k == 0, k == KC - 1, [skz, nbk(k, th)], [("ps", th)])
            for th in range(TH):
                act(zT[0:16, th * 512:(th + 1) * 512], psb[th][0:16, :], AF.Copy, [("ps", th)], [("zT",)])
            mark("z")
            for tt in range(NTT):
                bank = 2 + (tt % 2)
                pk = ("ps", bank)
                mm(psb[bank][:], zT[0:17, tt * 128:(tt + 1) * 128], wgate[0:17, :], True, True,
                   [("zT",), ("c", "wg")], [pk])
                t = mtmp[tt % 2]
                tk = ("mtmp", tt % 2)
                act(t, psb[bank][:], AF.Exp, [pk], [tk], scale=-1.0)
                act(t, t, AF.Ln, [tk], [tk], bias=1.0, scale=1.0)
                dve(lambda e, tt=tt, t=t: e.tensor_copy(out=sp_hi[:, tt, :], in_=t), [tk], [("sp", tt)])
                dve(lambda e, tt=tt, t=t: e.tensor_tensor(out=sp_lo[:, tt, :], in0=t, in1=sp_hi[:, tt, :],
                                                          op=ALU.subtract),
                    [tk, ("sp", tt)], [("spl", tt)])

            mark("sp")

            def make_evac_qk(dst, dkey, is_q):
                def evac(fc, banks):
                    hd = fc
                    for th in range(TH):
                        bb = 4 + th
                        for t4 in range(4):
                            tt = th * 4 + t4
                            mm(psb[bb][:, t4 * 128:(t4 + 1) * 128], sp_hi[:, tt, hd * 128:(hd + 1) * 128],
                               U_inc[:], True, False, [("sp", tt), ("c", "Ui")], [("ps", bb)])
                            mm(psb[bb][:, t4 * 128:(t4 + 1) * 128], sp_lo[:, tt, hd * 128:(hd + 1) * 128],
                               U_inc[:], False, True, [("spl", tt), ("c", "Ui")], [("ps", bb)])
                        t = mtmp[th]
                        tk = ("mtmp", th)
                        if is_q:
                            dve(lambda e, hd=hd, th=th, bb=bb: e.tensor_copy(
                                out=blast[:, hd, th * 4:(th + 1) * 4],
                                in_=psb[bb][:].rearrange("p (t c) -> p t c", c=128)[:, :, 127]),
                                [("ps", bb)], [("sm", "blast")])
                            act(t, psb[bb][:], AF.Exp, [("ps", bb)], [tk])
                            dve(lambda e, o=dst[:, hd, th * 512:(th + 1) * 512], ps=psb[banks[th]][:], t=t:
                                e.scalar_tensor_tensor(out=o, in0=ps, scalar=128.0 ** -0.5, in1=t,
                                                       op0=ALU.mult, op1=ALU.mult),
                                [("ps", banks[th]), tk], [(dkey, hd, th)])
                        else:
                            act(t, psb[bb][:], AF.Exp, [("ps", bb)], [tk], scale=-1.0)
                            dve(lambda e, o=dst[:, hd, th * 512:(th + 1) * 512], ps=psb[banks[th]][:], t=t:
                                e.tensor_tensor(out=o, in0=ps, in1=t, op=ALU.mult),
                                [("ps", banks[th]), tk], [(dkey, hd, th)])
                return evac

            proj_fm(wv, 2048, 4, make_evac_qk(qd, "qd", True))
            proj_fm(wv, 2560, 4, make_evac_qk(kd, "kd", False))
            act(dec.rearrange("p h t -> p (h t)"), blast.rearrange("p h t -> p (h t)"), AF.Exp,
                [("sm", "blast")], [("sm", "dec")])

            mark("qk")

            def evac_ktok(q, tt, ps, pk):
                pd = psb[6 + (tt % 2)][:, 0:256]
                pdk = ("ps", 6 + (tt % 2))
                mm(pd, U_str[:], sp_hi[:, tt, q * 256:(q + 1) * 256], True, False,
                   [("sp", tt), ("c", "Us")], [pdk])
                mm(pd, U_str[:], sp_lo[:, tt, q * 256:(q + 1) * 256], False, True,
                   [("spl", tt), ("c", "Us")], [pdk])
                t = mtmp[2][:, (tt % 2) * 256:(tt % 2 + 1) * 256]
                tk = ("mtmp2", tt % 2)
                act(t, pd, AF.Exp, [pdk], [tk])
                dve(lambda e, q=q, tt=tt, ps=ps, t=t: e.tensor_tensor(
                    out=kd_tok[:, tt, q * 256:(q + 1) * 256], in0=ps, in1=t, op=ALU.mult),
                    [pk, tk], [("kdt", tt)])

            proj_tok(wv, 2560, 2, evac_ktok)

            mark("ktok")

            def evac_vtok(q, tt, ps, pk):
                act(v_tok[:, tt, q * 256:(q + 1) * 256], ps, AF.Copy, [pk], [("vt", tt)])

            proj_tok(wv, 3072, 4, evac_vtok)

            mark("vtok")
            Sv = Sst[:, 0:1024].rearrange("p (h v) -> p h v", h=4)

            def state_update(tt, first):
                for hd in range(4):
                    bank = hd % 2
                    pS = psb[bank][:, 0:256]
                    pk = ("ps", bank)
                    mm(pS, kd_tok[:, tt, hd * 128:(hd + 1) * 128], v_tok[:, tt, hd * 256:(hd + 1) * 256],
                       True, True, [("kdt", tt), ("vt", tt)], [pk])
                    if first:
                        dve(lambda e, hd=hd, pS=pS: e.tensor_copy(out=Sv[:, hd, :], in_=pS),
                            [pk], [("S", hd)])
                    else:
                        dve(lambda e, hd=hd, pS=pS, tt=tt: e.scalar_tensor_tensor(
                            out=Sv[:, hd, :], in0=Sv[:, hd, :], scalar=dec[:, hd, tt:tt + 1], in1=pS,
                            op0=ALU.mult, op1=ALU.add),
                            [pk, ("S", hd), ("sm", "dec")], [("S", hd)])

            if not seq:
                for tt in range(NTT):
                    state_update(tt, tt == 0)
                dve(lambda e: e.tensor_reduce(out=Sst[:, 1024:1028], in_=blast, axis=AX.X, op=ALU.add),
                    [("sm", "blast")], [("S", "B")])
            skeys = [("S", hd) for hd in range(4)] + [("S", "B")]
            if seq:
                pass
            elif fused:
                spdma(cc_in[:, :], Sst[:, 0:PAYW], "cci", reads=skeys, writes=[("ccin",)])
                P.emit("pool", lambda e: e.collective_compute(
                    "AllGather", ALU.bypass, replica_groups=[list(range(NCORES))],
                    ins=[cc_in[:, :].opt()], outs=[cc_out[:, :].opt()]),
                    reads=[("ccin",)], writes=[("ccout",)], dma="cc", dma_inc=1)
                gsrc = cc_out
            else:
                if part == 1:
                    spdma(dout("payload", [128, PAYW])[:, :], Sst[:, 0:PAYW], "out", reads=skeys)
                P.muted = (part == 1)
                gsrc = din("gath", [NCORES * 128, PAYW])
                wv = wl("w_mix_in", l).rearrange("(k p) c -> p k c", p=128)

            P.guard([("rs", c, th) for c in range(8) for th in range(TH)])

            def evac_r(fc, banks):
                for th in range(TH):
                    act(r_s[:, fc, th * 512:(th + 1) * 512], psb[banks[th]][:], AF.Silu,
                        [("ps", banks[th])], [("rs", fc, th)])

            proj_fm(wv, 4096, 8, evac_r)

            P.guard([("Sbf", 0), ("Sbf", 1), ("stg",)] + [(n, i) for n in ("p2rs", "p2t", "osq", "scm") for i in range(2)])
            if seq:
                if cur_seg[0] == 0:
                    dve(lambda e: e.memset(Sst[:, 0:1024], 0.0), [], skeys)
                else:
                    spdma(Sst[:, 0:1024], gla_state[l], "stg", reads=[("gst", l)], writes=skeys)
            else:
                dve(lambda e: e.memset(Sst[:, 0:1024], 0.0), [], skeys)
            for j in (() if seq else (0, 1, 2, 4, 5, 6)):
                spdma(stg[:, 0:PAYW], gsrc[j * 128:(j + 1) * 128, :], "stg",
                      reads=[("ccout",)] if fused else [], writes=[("stg",)])
                selj = parT[:, 160 + j:161 + j]
                act(ea, stg[:, 1024:1028], AF.Exp, [("stg",)], [("sm", "ea")])
                dve(lambda e, selj=selj: e.tensor_scalar(out=a3, in0=ea, scalar1=-1.0, scalar2=selj,
                                                          op0=ALU.add, op1=ALU.mult),
                    [("sm", "ea"), ("c", "par")], [("sm", "a3")])
                dve(lambda e: e.tensor_scalar(out=a3, in0=a3, scalar1=1.0, scalar2=None, op0=ALU.add),
                    [("sm", "a3")], [("sm", "a3")])
                for hd in range(4):
                    sj = stg[:, hd * 256:(hd + 1) * 256]
                    dve(lambda e, sj=sj, selj=selj: e.tensor_scalar(out=sj, in0=sj, scalar1=selj,
                                                                    scalar2=None, op0=ALU.mult),
                        [("stg",), ("c", "par")], [("stg",)])
                    dve(lambda e, hd=hd, sj=sj: e.scalar_tensor_tensor(
                        out=Sv[:, hd, :], in0=Sv[:, hd, :], scalar=a3[:, hd:hd + 1], in1=sj,
                        op0=ALU.mult, op1=ALU.add),
                        [("stg",), ("sm", "a3"), ("S", hd)], [("S", hd)])

            act(S_bf[0], Sst[:, 0:1024], AF.Copy, [("S", hd) for hd in range(4)], [("Sbf", 0)])
            og0 = 144 + l * 8
            items = [(tt, hd) for tt in range(NTT) for hd in range(4)]
            nit = len(items)
            PO_BANKS = (2, 3, 7)

            def emit_sc(i):
                tt, hd = items[i]
                th = tt // 4
                tsl = slice(tt * 128, (tt + 1) * 128)
                i2 = i % 2
                psc = psb[i2][:, 0:128]
                kpsc = ("ps", i2)
                mm(psc, kd[:, hd, tsl], qd[:, hd, tsl], True, True,
                   [("kd", hd, th), ("qd", hd, th)], [kpsc])
                sc = scm[i2]
                dve(lambda e, sc=sc, psc=psc: e.tensor_tensor(out=sc, in0=psc, in1=maskT[:], op=ALU.mult),
                    [kpsc, ("c", "mk")], [("scm", i2)])

            def emit_o(i):
                tt, hd = items[i]
                th = tt // 4
                tsl = slice(tt * 128, (tt + 1) * 128)
                i2 = i % 2
                sc = scm[i2]
                ksc = ("scm", i2)
                pb = PO_BANKS[i % 3]
                po = psb[pb][:, 0:256]
                kpo = ("ps", pb)
                cur = tt % 2
                for vc in range(2):
                    vs = slice(hd * 256 + vc * 128, hd * 256 + (vc + 1) * 128)
                    mm(po[:, vc * 128:(vc + 1) * 128], v_tok[:, tt, vs], sc, True, False,
                       [("vt", tt), ksc], [kpo])
                    mm(po[:, vc * 128:(vc + 1) * 128], S_bf[cur][:, vs], qd[:, hd, tsl], False, True,
                       [("Sbf", cur), ("qd", hd, th)], [kpo])
                act(p2b[i2], po, AF.Square, [kpo], [("osq", i2)])

            def emit_ss(i):
                tt, hd = items[i]
                th = tt // 4
                tsl = slice(tt * 128, (tt + 1) * 128)
                i2 = i % 2
                pb = PO_BANKS[i % 3]
                po = psb[pb][:, 0:256]
                kpo = ("ps", pb)
                osq = p2b[i2]
                kosq = ("osq", i2)
                pss = psb[4 + i2][:, 0:128]
                kpss = ("ps", 4 + i2)
                mm(pss, ones_bf[:], osq[:, 0:128], True, False, [kosq, ("c", "ones")], [kpss])
                mm(pss, ones_bf[:], osq[:, 128:256], False, True, [kosq], [kpss])
                rs = p2f[i2]
                krs = ("p2rs", i2)
                act(rs, pss, AF.Ln, [kpss, ("c", "eps")], [krs], bias=epsT[:], scale=1.0 / 256.0)
                act(rs, rs, AF.Exp, [krs], [krs], scale=-0.5)
                for vc in range(2):
                    c = hd * 2 + vc
                    tq = p2f[2 + vc]
                    ktq = ("p2t", vc)
                    dve(lambda e, tq=tq, po=po, vc=vc, c=c, rs=rs: e.scalar_tensor_tensor(
                        out=tq, in0=po[:, vc * 128:(vc + 1) * 128],
                        scalar=parT[:, og0 + c:og0 + c + 1], in1=rs, op0=ALU.mult, op1=ALU.mult),
                        [kpo, krs, ("c", "par")], [ktq])
                    ys = r_s[:, c, tsl]
                    dve(lambda e, ys=ys, tq=tq: e.tensor_tensor(out=ys, in0=tq, in1=ys, op=ALU.mult),
                        [ktq, ("rs", c, th)], [("rs", c, th)])

            def emit_upd(tt):
                if not (tt < NTT - 1 or (seq and cur_seg[0] < NSEG - 1)):
                    return
                for hd in range(4):
                    pS = psb[6][:, 0:256]
                    pk = ("ps", 6)
                    mm(pS, kd_tok[:, tt, hd * 128:(hd + 1) * 128], v_tok[:, tt, hd * 256:(hd + 1) * 256],
                       True, True, [("kdt", tt), ("vt", tt)], [pk])
                    dve(lambda e, hd=hd, pS=pS, tt=tt: e.scalar_tensor_tensor(
                        out=Sv[:, hd, :], in0=Sv[:, hd, :], scalar=dec[:, hd, tt:tt + 1], in1=pS,
                        op0=ALU.mult, op1=ALU.add),
                        [pk, ("S", hd), ("sm", "dec")], [("S", hd)])
                if tt < NTT - 1:
                    act(S_bf[(tt + 1) % 2], Sst[:, 0:1024], AF.Copy, [("S", hd) for hd in range(4)],
                        [("Sbf", (tt + 1) % 2)])

            for step in range(nit + 2):
                if step < nit:
                    emit_sc(step)
                if 1 <= step <= nit:
                    emit_o(step - 1)
                if 2 <= step <= nit + 1:
                    emit_ss(step - 2)
                if step < nit and items[step][1] == 0:
                    emit_upd(items[step][0])
            if seq and cur_seg[0] < NSEG - 1:
                gtok = spdma(gla_state[l], Sst[:, 0:1024], "gst", reads=[("S", hd) for hd in range(4)],
                             writes=[("gst", l)])
                P.guard([("B", 0), ("B", 1)], [gtok])
            mix_out_half(l, 1024, r_s, "rs")
            P.muted = False
            return False

        def ple_stage(l):
            P.guard([("pTb",)])
            sg0 = cur_seg[0] * NT
            d_pT = wl("pT", l).rearrange("(k p) t -> p k t", p=128)[:, :, sg0:sg0 + NT]
            P.emit("pool", lambda e: e.dma_start(out=pTb[:], in_=d_pT), writes=[("pTb",)], dma="pin")
            si = next_aslot()
            wp = aslot(si, "p (k c) -> p k c", k=2)
            wpk = ("A", si)
            wdma(wp[:], wl("w_ple_proj", l).rearrange("(k p) c -> p k c", p=128), [wpk])
            wv = wl("w_ple_gate", l).rearrange("(k p) c -> p k c", p=128)

            def evac(fc, banks):
                pb = (4, 5) if fc % 2 == 0 else (6, 7)
                for k2 in range(2):
                    for th in range(TH):
                        mm(psb[pb[th]][:], wp[:, k2, fc * 128:(fc + 1) * 128], pTb[:, k2, th * 512:(th + 1) * 512],
                           k2 == 0, k2 == 1, [wpk, ("pTb",)], [("ps", pb[th])])
                for th in range(TH):
                    i = (fc * 2 + th) % 4
                    t = ftmp[i]
                    tk = ("ftmp", i)
                    act(t, psb[banks[th]][:], AF.Sigmoid, [("ps", banks[th])], [tk])
                    dve(lambda e, t=t, ps=psb[pb[th]][:]: e.tensor_tensor(out=t, in0=ps, in1=t, op=ALU.mult),
                        [("ps", pb[th]), tk], [tk])
                    hs = h[:, fc, th * 512:(th + 1) * 512]
                    dve(lambda e, hs=hs, t=t: e.tensor_tensor(out=hs, in0=hs, in1=t, op=ALU.add),
                        [tk, hk(fc, th)], [hk(fc, th)])

            astate["keep"] = si
            proj_fm(wv, 0, KC, evac)
            astate["keep"] = None

        ffn_keys = [("actb", s, fc, th) for s in range(2) for fc in range(GC) for th in range(TH)] + \
                   [("ftmp", i) for i in range(4)]

        def save_state():
            toks = P.snapshot()
            hv = dout("h_out", [128, KC * NT])
            for k in range(KC):
                spdma(hv[:, k * NT:(k + 1) * NT], h[:, k, :], "out", extra=toks)
            spdma(dout("nb_out", [128, KC * NT], BF16)[:, :], nb[:].rearrange("p k t -> p (k t)"), "out",
                  extra=toks)
            spdma(dout("r1_out", [128, 20480], BF16)[:, :], R1[:], "out", extra=toks)
            spdma(dout("sm_out", [128, 256])[:, :], small[:], "out", extra=toks)

        def restore_state():
            hv = din("h_sv", [128, KC * NT])
            for k in range(KC):
                spdma(h[:, k, :], hv[:, k * NT:(k + 1) * NT], "rst")
            spdma(nb[:].rearrange("p k t -> p (k t)"), din("nb_sv", [128, KC * NT], BF16)[:, :], "rst")
            spdma(R1[:], din("r1_sv", [128, 20480], BF16)[:, :], "rst")
            spdma(small[:], din("sm_sv", [128, 256])[:, :], "rst")
            total = P.dcnt["rst"]
            for e in P.ops:
                P.ops[e].append(([("rst", total)], None, None))
                P.waited[e]["rst"] = total

        def layer_a(l, part):
            P.guard(ffn_keys + [("B", 0), ("B", 1)])
            norm_stage((l * 4 + 0) * 16)
            ffn_stage(wl("w_ffn1_in", l), wl("w_ffn1_out", l))
            norm_stage((l * 4 + 1) * 16)
            mixer_stage(l, part)

        def layer_b(l):
            P.guard(ffn_keys + [("B", 0), ("B", 1)])
            norm_stage((l * 4 + 2) * 16)
            ffn_stage(wl("w_ffn2_in", l), wl("w_ffn2_out", l))
            norm_stage((l * 4 + 3) * 16)
            ple_stage(l)

        def store_out(seg=0):
            ov = dout("outT", [D, SEQT]).rearrange("(k p) t -> p k t", p=128)
            for k in range(KC):
                spdma(ov[:, k, seg * NT:(seg + 1) * NT], h[:, k, :], "out", reads=[hk(k, 0), hk(k, 1)])
            for k in range(KC):
                for th in range(TH):
                    P.res[hk(k, th)][1]["out"] = P.dcnt["out"]

        if seq:
            for seg in range(NSEG):
                cur_seg[0] = seg
                load_x(seg)
                for l in range(DEPTH):
                    layer_a(l, 0)
                    layer_b(l)
                norm_stage(8 * 16, final=True)
                store_out(seg)
        elif fused:
            load_x()
            for l in range(DEPTH):
                layer_a(l, 0)
                layer_b(l)
            norm_stage(8 * 16, final=True)
            store_out()
        elif phase == 0:
            load_x()
            layer_a(0, 1)
            save_state()
        elif phase == 1:
            restore_state()
            mixer_stage(0, 2)
            layer_b(0)
            layer_a(1, 1)
            save_state()
        else:
            restore_state()
            mixer_stage(1, 2)
            layer_b(1)
            norm_stage(8 * 16, final=True)
            store_out()
        P.ops["sp"].append(([("out", P.dcnt["out"])], None, None))

        def runner(name):
            def body(eng):
                for waits, fn, inc in P.ops[name]:
                    for (k, v) in waits:
                        eng.wait_ge(sems[k], v)
                    if fn is None:
                        continue
                    ins = fn(eng)
                    if inc is not None:
                        ins.then_inc(sems[inc[0]], inc[1])
            return body

        block.sync(runner("sp"))
        block.gpsimd(runner("pool"))
        block.scalar(runner("act"))
        block.vector(runner("dve"))
        block.tensor(runner("pe"))
    return nc, list(needed.keys()), list(outs_decl.keys())


_CACHE = {}
MODE = "seq"


def _make_provider(inputs):
    x = np.asarray(inputs["x"], dtype=np.float32).reshape(-1, D)
    p = np.asarray(inputs["p"], dtype=np.float32).reshape(DEPTH, -1, 256)
    gains = []
    for l in range(DEPTH):
        for n in ("ffn1_norm", "mix_norm", "ffn2_norm", "ple_norm"):
            gains.append(np.asarray(inputs[n], dtype=np.float32)[l].reshape(KC, 128).T)
    gains.append(np.asarray(inputs["final_norm"], dtype=np.float32).reshape(KC, 128).T)
    og = [np.asarray(inputs["gla_o_gain"], dtype=np.float32)[l].reshape(8, 128).T for l in range(DEPTH)]
    shared = {}

    def get(name, c, extra):
        if name in extra:
            return extra[name][c]
        if name == "xT":
            return np.ascontiguousarray(x[c * NT:(c + 1) * NT].T)
        if name == "par":
            sel = np.zeros((128, 8), np.float32)
            for j in range(NCORES):
                if j // 4 == c // 4 and j < c:
                    sel[:, j] = 1.0
            return np.ascontiguousarray(np.concatenate(gains + og + [sel], axis=1).astype(np.float32))
        base, l = name.rsplit("_", 1)
        l = int(l)
        if base == "pT":
            return np.ascontiguousarray(p[l, c * NT:(c + 1) * NT, :].T)
        if name not in shared:
            if base == "sg_wT":
                a = np.asarray(inputs["sg_w"], dtype=np.float32)[l].transpose(2, 0, 1).reshape(128, 1024)
            elif base in ("sg_b", "sg_v_gain", "gla_b_gate"):
                a = np.asarray(inputs[base], dtype=np.float32)[l].reshape(1, -1)
            else:
                a = np.asarray(inputs[base], dtype=np.float32)[l]
            shared[name] = np.ascontiguousarray(a)
        return shared[name]

    return get


def kernel(**inputs):
    get = _make_provider(inputs)
    cores = list(range(NCORES))
    if MODE == "seq":
        if "seq" not in _CACHE:
            _CACHE["seq"] = build_program(seq=True)
        nc, needed, _ = _CACHE["seq"]
        x = np.asarray(inputs["x"], dtype=np.float32)
        p = np.asarray(inputs["p"], dtype=np.float32)
        in_maps = []
        for b in range(2):
            m = {}
            for n in needed:
                if n == "xT":
                    m[n] = np.ascontiguousarray(x[b].T)
                elif n.startswith("pT_"):
                    m[n] = np.ascontiguousarray(p[int(n[3:]), b].T)
                else:
                    m[n] = get(n, 0, {})
            in_maps.append(m)
        res = run_bass_kernel_spmd(nc, in_maps, core_ids=[0, 1])
        out = np.stack([np.asarray(r["outT"]).T for r in res.results], axis=0)
        return np.ascontiguousarray(out.astype(np.float32))
    if MODE == "fused":
        if "fused" not in _CACHE:
            _CACHE["fused"] = build_program()
        nc, needed, _ = _CACHE["fused"]
        in_maps = [{n: get(n, c, {}) for n in needed} for c in cores]
        res = run_bass_kernel_spmd(nc, in_maps, core_ids=cores)
    else:
        extra = {}
        for ph in range(3):
            key = ("phase", ph)
            if key not in _CACHE:
                _CACHE[key] = build_program(phase=ph)
            nc, needed, _ = _CACHE[key]
            in_maps = [{n: get(n, c, extra) for n in needed} for c in cores]
            res = run_bass_kernel_spmd(nc, in_maps, core_ids=cores)
            if ph < 2:
                extra = {k + "_sv": [np.asarray(res.results[c][k + "_out"]) for c in cores]
                         for k in ("h", "nb", "r1", "sm")}
                gath = np.ascontiguousarray(
                    np.concatenate([np.asarray(res.results[c]["payload"]) for c in cores], axis=0))
                extra["gath"] = [gath] * NCORES
    outs = [np.asarray(r["outT"]).T for r in res.results]
    out = np.concatenate(outs, axis=0).reshape(2, 4096, D).astype(np.float32)
    return out
```

```python
import numpy as np
import concourse.bass as bass
import concourse.mybir as mybir
from concourse.bass_utils import run_bass_kernel_spmd

F32 = mybir.dt.float32
BF16 = mybir.dt.bfloat16
AF = mybir.ActivationFunctionType
ALU = mybir.AluOpType
AX = mybir.AxisListType

NCORES = 8
D = 2048
KC = 16
NT = 1024
TH = 2
NTT = 8
DFF = 5632
NFC = 44
GC = 4
NG = 11
DEPTH = 2
INW = 5136
EPS = 1e-6
NPAR = 9 * 16 + 16 + 8
PAYW = 1028


class Plan:
    CE = ("pe", "act", "dve", "pool")

    def __init__(self):
        self.ops = {e: [] for e in ("pe", "act", "dve", "pool", "sp")}
        self.cnt = {e: 0 for e in self.CE}
        self.pend = {e: False for e in self.CE}
        self.waited = {e: {} for e in self.ops}
        self.res = {}
        self.dcnt = {}
        self.muted = False

    def _res(self, k):
        r = self.res.get(k)
        if r is None:
            r = self.res[k] = [None, {}]
        return r

    def snapshot(self):
        return [(e, self.cnt[e] + (1 if self.pend[e] else 0)) for e in self.CE
                if self.cnt[e] or self.pend[e]]

    def guard(self, keys, toks=None):
        if self.muted:
            return
        if toks is None:
            toks = self.snapshot()
        for k in keys:
            R = self._res(k)
            for (s, v) in toks:
                if R[1].get(s, 0) < v:
                    R[1][s] = v

    def emit(self, e, fn, reads=(), writes=(), extra=(), signal=True, dma=None, dma_inc=16):
        if self.muted:
            return None
        deps = {}
        own_raw = 0

        def add(tok, raw=False):
            nonlocal own_raw
            if tok is None:
                return
            k, v = tok
            if k == e:
                if raw and v > own_raw:
                    own_raw = v
                return
            if v > deps.get(k, 0):
                deps[k] = v

        for r in reads:
            R = self._res(r)
            add(R[0], True)
        for w in writes:
            R = self._res(w)
            add(R[0])
            for k, v in R[1].items():
                add((k, v))
        for t in extra:
            add(t, True)
        wd = self.waited[e]
        waits = []
        for k, v in deps.items():
            if wd.get(k, 0) < v:
                wd[k] = v
                waits.append((k, v))
        if own_raw and e in ("act", "dve", "pool") and wd.get(e, 0) < own_raw:
            wd[e] = own_raw
            waits.append((e, own_raw))
        if dma is not None:
            self.dcnt[dma] = self.dcnt.get(dma, 0) + dma_inc
            tok = (dma, self.dcnt[dma])
            inc = (dma, dma_inc)
        elif signal:
            self.cnt[e] += 1
            self.pend[e] = False
            tok = (e, self.cnt[e])
            inc = (e, 1)
        else:
            self.pend[e] = True
            tok = (e, self.cnt[e] + 1)
            inc = None
        k, v = tok
        for r in reads:
            R = self._res(r)
            if R[1].get(k, 0) < v:
                R[1][k] = v
        for w in writes:
            R = self._res(w)
            R[0] = tok
            R[1] = {}
        self.ops[e].append((waits, fn, inc))
        return tok


def build_program(stop_after=None, phase=None, seq=False):
    nc = bass.Bass("TRN2", target_bir_lowering=False)
    P = Plan()
    dt = nc.dram_tensor
    needed = {}
    outs_decl = {}

    dummies = {}

    def din(name, shape, dtype=F32):
        if name in needed:
            return needed[name]
        if P.muted:
            if name not in dummies:
                dummies[name] = dt("dmy_" + name, list(shape), dtype).ap()
            return dummies[name]
        needed[name] = dt(name, list(shape), dtype, kind="ExternalInput").ap()
        return needed[name]

    def dout(name, shape, dtype=F32):
        if name not in outs_decl:
            outs_decl[name] = dt(name, list(shape), dtype, kind="ExternalOutput").ap()
        return outs_decl[name]

    LSHAPES = {
        "w_ffn1_in": [D, 2 * DFF], "w_ffn1_out": [DFF, D], "w_mix_in": [D, INW], "w_mix_out": [D, D],
        "w_ffn2_in": [D, 2 * DFF], "w_ffn2_out": [DFF, D], "w_ple_gate": [D, D], "w_ple_proj": [256, D],
        "sg_v_gain": [1, 1024], "sg_wT": [128, 1024], "sg_b": [1, 1024], "gla_w_gate": [16, 512],
        "gla_b_gate": [1, 512], "pT": [256, NT],
    }

    def wl(name, l):
        return din(f"{name}_{l}", LSHAPES[name])

    NSEG = 4 if seq else 1
    SEQT = NT * NSEG
    if seq:
        LSHAPES["pT"] = [256, SEQT]
        gla_state = dt("gla_state", [DEPTH, 128, 1024], F32)
    cur_seg = [0]
    fused = (phase is None) and not seq
    if fused:
        cc_in = dt("cc_in", [128, PAYW], F32)
        cc_out = dt("cc_out", [NCORES * 128, PAYW], F32)

    sb = nc.sbuf_tensor
    import contextlib
    es = contextlib.ExitStack()
    with es:
        ent = es.enter_context
        h = ent(sb("h", [128, KC, NT], F32))
        nb = ent(sb("nb", [128, KC, NT], BF16))
        aslots = ent(sb("aslots", [128, 3, KC * 256], BF16))
        R2 = ent(sb("R2", [128, 16384], BF16))
        R1 = ent(sb("R1", [128, 20480], BF16))
        ones_bf = ent(sb("ones_bf", [128, 128], BF16))
        U_inc = ent(sb("U_inc", [128, 128], BF16))
        U_str = ent(sb("U_str", [128, 128], BF16))
        maskT = ent(sb("maskT", [128, 128], F32))
        epsT = ent(sb("epsT", [128, 1], F32))
        parT = ent(sb("parT", [128, NPAR], F32))
        wTm = ent(sb("wTm", [128, 8, 128], BF16))
        bs_row = ent(sb("bs_row", [1, 1024], BF16))
        wgate = ent(sb("wgate", [17, 512], BF16))
        vgain = ent(sb("vgain", [128, 1024], F32))
        small = ent(sb("small", [128, 256], F32))
        psb = [ent(nc.psum_tensor(f"ps{i}", [128, 512], F32)) for i in range(8)]
        sem_names = ["pe", "act", "dve", "pool", "xin", "par", "lwT", "lbs", "lwg", "lvg", "pin", "A0", "A1", "A2",
                     "B0", "B1", "cci", "cc", "stg", "out", "rst", "gst"]
        sems = {n: ent(nc.semaphore(n)) for n in sem_names}
        block = ent(nc.Block())

        def carve(reg, off, nbytes, dtype=BF16, pat=None, **kw):
            v = reg[:, off // 2:(off + nbytes) // 2]
            if dtype == F32:
                v = v.bitcast(F32)
            if pat:
                v = v.rearrange(pat, **kw)
            return v

        KB = 1024
        actb = [carve(R1, i * 8 * KB, 8 * KB, BF16, "p (c t) -> p c t", c=GC) for i in range(2)]
        ftmp = [carve(R1, 16 * KB + i * 2 * KB, 2 * KB, F32) for i in range(4)]
        wout = [carve(R2, i * 16 * KB, 16 * KB, BF16, "p (c d) -> p c d", c=GC) for i in range(2)]
        u_g = carve(R1, 0, 16 * KB, BF16, "p (c t) -> p c t", c=8)
        v_ln = carve(R1, 16 * KB, 16 * KB, BF16, "p (t c) -> p t c", t=NTT)
        vgt = [carve(R1, 32 * KB + i * KB, KB, F32) for i in range(4)]
        sqt = [carve(R1, 36 * KB + i * KB, KB, F32) for i in range(2)]
        qd = carve(R1, 0, 8 * KB, BF16, "p (c t) -> p c t", c=4)
        kd = carve(R1, 8 * KB, 8 * KB, BF16, "p (c t) -> p c t", c=4)
        kd_tok = carve(R1, 16 * KB, 8 * KB, BF16, "p (t c) -> p t c", t=NTT)
        v_tok = carve(R1, 24 * KB, 16 * KB, BF16, "p (t c) -> p t c", t=NTT)
        sp_hi = carve(R2, 0, 8 * KB, BF16, "p (t c) -> p t c", t=NTT)
        sp_lo = carve(R2, 8 * KB, 8 * KB, BF16, "p (t c) -> p t c", t=NTT)
        zT = carve(R2, 16 * KB, 2 * KB, BF16)
        mtmp = [carve(R2, 18 * KB + i * 2 * KB, 2 * KB, F32) for i in range(3)]
        r_s = carve(R2, 0, 16 * KB, BF16, "p (c t) -> p c t", c=8)
        S_bf = [carve(R2, 16 * KB + i * 2 * KB, 2 * KB, BF16) for i in range(2)]
        p2f = [carve(R2, 20 * KB + i * 512, 512, F32) for i in range(4)]
        p2b = [carve(R2, 22 * KB + i * 512, 512, BF16) for i in range(2)]
        scm = [carve(R2, 23 * KB + i * 256, 256, BF16) for i in range(2)]
        stg = carve(R2, 23 * KB + 512, 4128, F32)
        Sst = carve(R2, 28 * KB - 32, 4128, F32)
        pTb = carve(R1, 24 * KB, 4 * KB, BF16, "p (k t) -> p k t", k=2)

        def aslot(i, pat=None, **kw):
            v = aslots[:, i, :]
            if pat:
                v = v.rearrange(pat, **kw)
            return v

        blast = small[:, 32:64].rearrange("p (h t) -> p h t", h=4)
        dec = small[:, 64:96].rearrange("p (h t) -> p h t", h=4)
        ea = small[:, 96:100]
        a3 = small[:, 100:104]

        astate = {"i": 0, "keep": None}

        def next_aslot():
            while True:
                i = astate["i"] % 3
                astate["i"] += 1
                if i != astate["keep"]:
                    return i

        def mm(out, lhsT, rhs, start, stop, reads, writes, extra=(), sig=False):
            return P.emit("pe", lambda e: e.matmul(out, lhsT=lhsT, rhs=rhs, start=start, stop=stop),
                          reads=reads, writes=writes, extra=extra, signal=bool(stop) or sig)

        def act(out, in_, func, reads, writes, bias=None, scale=None):
            kw = {}
            if bias is not None:
                kw["bias"] = bias
            if scale is not None:
                kw["scale"] = scale
            return P.emit("act", lambda e: e.activation(out=out, in_=in_, func=func, **kw),
                          reads=reads, writes=writes)

        def dve(fn, reads, writes):
            return P.emit("dve", fn, reads=reads, writes=writes)

        def wdma(out, in_, writes):
            slotkey = writes[0]
            semn = slotkey[0] + str(slotkey[1])
            return P.emit("pool", lambda e: e.dma_start(out=out, in_=in_), writes=writes, dma=semn)

        def spdma(out, in_, sem, reads=(), writes=(), extra=()):
            return P.emit("sp", lambda e: e.dma_start(out=out, in_=in_), reads=reads, writes=writes,
                          extra=extra, dma=sem)

        def hk(k, th):
            return ("h", k, th)

        def nbk(k, th):
            return ("nb", k, th)

        def pool_op(fn, writes, reads=()):
            return P.emit("pool", fn, reads=reads, writes=writes)

        pool_op(lambda e: e.memset(small[:], 0.0), [("sm", "blast"), ("sm", "dec"), ("sm", "ea"), ("sm", "a3")])
        pool_op(lambda e: e.memset(ones_bf[:], 1.0), [("c", "ones")])
        pool_op(lambda e: e.memset(epsT[:], EPS), [("c", "eps")])
        pool_op(lambda e: e.memset(U_inc[:], -1.0 / 16.0), [("c", "Ui")])
        pool_op(lambda e: e.affine_select(out=U_inc[:], in_=U_inc[:], pattern=[[1, 128]],
                                          compare_op=ALU.is_ge, fill=0.0, base=0, channel_multiplier=-1),
                [("c", "Ui")], [("c", "Ui")])
        pool_op(lambda e: e.memset(U_str[:], -1.0 / 16.0), [("c", "Us")])
        pool_op(lambda e: e.affine_select(out=U_str[:], in_=U_str[:], pattern=[[-1, 128]],
                                          compare_op=ALU.is_gt, fill=0.0, base=0, channel_multiplier=1),
                [("c", "Us")], [("c", "Us")])
        pool_op(lambda e: e.memset(maskT[:], 1.0), [("c", "mk")])
        pool_op(lambda e: e.affine_select(out=maskT[:], in_=maskT[:], pattern=[[1, 128]],
                                          compare_op=ALU.is_ge, fill=0.0, base=0, channel_multiplier=-1),
                [("c", "mk")], [("c", "mk")])
        CONST = [("c", "ones"), ("c", "eps"), ("c", "Ui"), ("c", "Us"), ("c", "mk"), ("c", "par")]

        spdma(parT[:], din("par", [128, NPAR])[:, :], "par", writes=[("c", "par")])
        def load_x(seg=0):
            xv = din("xT", [D, SEQT]).rearrange("(k p) t -> p k t", p=128)
            for k in range(KC):
                spdma(h[:, k, :], xv[:, k, seg * NT:(seg + 1) * NT], "xin", writes=[hk(k, 0), hk(k, 1)])
            for k in range(KC):
                for th in range(TH):
                    P.res[hk(k, th)][0] = ("xin", 16 * KC * (seg + 1))

        def norm_stage(gcol, final=False):
            for th in range(TH):
                ts = slice(th * 512, (th + 1) * 512)
                bank = 6 + th
                for k in range(KC):
                    t = ftmp[k % 2].bitcast(BF16)[:, 0:512]
                    tk = ("ftmp", k % 2)
                    act(t, h[:, k, ts], AF.Square, [hk(k, th)], [tk])
                    mm(psb[bank][:], ones_bf[:], t, k == 0, k == KC - 1,
                       [tk, ("c", "ones")], [("ps", bank)], sig=True)
                rt = ftmp[2 + th]
                rk = ("ftmp", 2 + th)
                act(rt, psb[bank][:], AF.Ln, [("ps", bank), ("c", "eps")], [rk], bias=epsT[:], scale=1.0 / D)
                act(rt, rt, AF.Exp, [rk], [rk], scale=-0.5)
                for k in range(KC):
                    g = parT[:, gcol + k:gcol + k + 1]
                    if final:
                        o = h[:, k, ts]
                        wr = [hk(k, th)]
                    else:
                        o = nb[:, k, ts]
                        wr = [nbk(k, th)]
                    dve(lambda e, o=o, i0=h[:, k, ts], g=g, rt=rt: e.scalar_tensor_tensor(
                        out=o, in0=i0, scalar=g, in1=rt, op0=ALU.mult, op1=ALU.mult),
                        [hk(k, th), rk, ("c", "par")], wr)

        def ffn_stage(w_in, w_out):
            wiv = w_in.rearrange("(k p) c -> p k c", p=128)
            wov = w_out.rearrange("(c p) d -> p c d", p=128)
            ti = [0]

            def p1(g):
                ab = actb[g % 2]
                for fc in range(GC):
                    f = g * GC + fc
                    si = next_aslot()
                    sl = aslot(si, "p (k c) -> p k c", k=KC)
                    sk = ("A", si)
                    wdma(sl[:, :, 0:128], wiv[:, :, f * 128:(f + 1) * 128], [sk])
                    wdma(sl[:, :, 128:256], wiv[:, :, DFF + f * 128:DFF + (f + 1) * 128], [sk])
                    for half, banks in ((0, (0, 1)), (1, (2, 3))):
                        for k in range(KC):
                            for th in range(TH):
                                mm(psb[banks[th]][:], sl[:, k, half * 128:(half + 1) * 128],
                                   nb[:, k, th * 512:(th + 1) * 512], k == 0, k == KC - 1,
                                   [sk, nbk(k, th)], [("ps", banks[th])])
                    for th in range(TH):
                        t = ftmp[ti[0] % 4]
                        tk = ("ftmp", ti[0] % 4)
                        ti[0] += 1
                        act(t, psb[th][:], AF.Silu, [("ps", th)], [tk])
                        dve(lambda e, o=ab[:, fc, th * 512:(th + 1) * 512], i0=psb[2 + th][:], t=t:
                            e.tensor_tensor(out=o, in0=i0, in1=t, op=ALU.mult),
                            [("ps", 2 + th), tk], [("actb", g % 2, fc, th)])

            def p2(g):
                ab = actb[g % 2]
                bi = g % 2
                bk = ("B", bi)
                for j in range(KC):
                    banks = (4, 5) if j % 2 == 0 else (6, 7)
                    for fc in range(GC):
                        for th in range(TH):
                            mm(psb[banks[th]][:], wout[bi][:, fc, j * 128:(j + 1) * 128],
                               ab[:, fc, th * 512:(th + 1) * 512], fc == 0, fc == GC - 1,
                               [bk, ("actb", g % 2, fc, th)], [("ps", banks[th])])
                    for th in range(TH):
                        hs = h[:, j, th * 512:(th + 1) * 512]
                        dve(lambda e, hs=hs, ps=psb[banks[th]][:]: e.scalar_tensor_tensor(
                            out=hs, in0=ps, scalar=0.5, in1=hs, op0=ALU.mult, op1=ALU.add),
                            [("ps", banks[th]), hk(j, th)], [hk(j, th)])

            def loadB(g):
                wdma(wout[g % 2][:], wov[:, g * GC:(g + 1) * GC, :], [("B", g % 2)])

            p1(0)
            loadB(0)
            for g in range(1, NG):
                p1(g)
                p2(g - 1)
                if g < NG:
                    loadB(g)
            p2(NG - 1)

        def proj_fm(wv, c0, nchunks, evac, nk=KC, rhs_fn=None, rkey_fn=None):
            if rhs_fn is None:
                rhs_fn = lambda k, th: nb[:, k, th * 512:(th + 1) * 512]
                rkey_fn = nbk
            cpp = (KC * 256) // (nk * 128)
            for s0 in range(0, nchunks, cpp):
                n = min(cpp, nchunks - s0)
                si = next_aslot()
                sl = aslot(si, "p (k c) -> p k c", k=nk)
                sk = ("A", si)
                wdma(sl[:, :, 0:n * 128], wv[:, :, c0 + s0 * 128:c0 + (s0 + n) * 128], [sk])
                for j in range(n):
                    fc = s0 + j
                    banks = (0, 1) if fc % 2 == 0 else (2, 3)
                    for k in range(nk):
                        for th in range(TH):
                            mm(psb[banks[th]][:], sl[:, k, j * 128:(j + 1) * 128], rhs_fn(k, th),
                               k == 0, k == nk - 1, [sk, rkey_fn(k, th)], [("ps", banks[th])])
                    evac(fc, banks)

        def proj_tok(wv, c0, nslots, evac):
            for q in range(nslots):
                si = next_aslot()
                sl = aslot(si, "p (k c) -> p k c", k=KC)
                sk = ("A", si)
                wdma(sl[:, :, :], wv[:, :, c0 + q * 256:c0 + (q + 1) * 256], [sk])
                for tt in range(NTT):
                    bank = 4 + (tt % 2)
                    pk = ("ps", bank)
                    o = psb[bank][:, 0:256]
                    th = tt // 4
                    for k in range(KC):
                        mm(o, nb[:, k, tt * 128:(tt + 1) * 128], sl[:, k, :], k == 0, k == KC - 1,
                           [sk, nbk(k, th)], [pk])
                    evac(q, tt, o, pk)

        def mix_out_half(l, r0, ybuf, ykey):
            wv = wl("w_mix_out", l)[r0:r0 + 1024, :].rearrange("(k p) c -> p k c", p=128)

            def evac(fc, banks):
                for th in range(TH):
                    hs = h[:, fc, th * 512:(th + 1) * 512]
                    dve(lambda e, hs=hs, ps=psb[banks[th]][:]: e.tensor_tensor(
                        out=hs, in0=ps, in1=hs, op=ALU.add),
                        [("ps", banks[th]), hk(fc, th)], [hk(fc, th)])

            proj_fm(wv, 0, KC, evac, nk=8,
                    rhs_fn=lambda k, th: ybuf[:, k, th * 512:(th + 1) * 512],
                    rkey_fn=lambda k, th: (ykey, k, th))

        def mark(name):
            if stop_after == name:
                P.muted = True

        def mixer_stage(l, part=0):
            P.muted = (part == 2)
            wv = wl("w_mix_in", l).rearrange("(k p) c -> p k c", p=128)
            d_wT, d_sb, d_wg, d_bg = wl("sg_wT", l), wl("sg_b", l), wl("gla_w_gate", l), wl("gla_b_gate", l)
            P.emit("pool", lambda e: e.dma_start(out=wTm[:].rearrange("p h t -> p (h t)"), in_=d_wT),
                   writes=[("c", "wTm")], dma="lwT")
            P.emit("pool", lambda e: e.dma_start(out=bs_row[:], in_=d_sb),
                   writes=[("c", "bs")], dma="lbs")
            P.emit("pool", lambda e: e.dma_start(out=wgate[0:16, :], in_=d_wg),
                   writes=[("c", "wg")], dma="lwg")
            P.emit("pool", lambda e: e.dma_start(out=wgate[16:17, :], in_=d_bg),
                   writes=[("c", "wg")], dma="lwg")
            P.emit("pool", lambda e: e.affine_select(out=wTm[:], in_=wTm[:], pattern=[[0, 8], [1, 128]],
                                                     compare_op=ALU.is_ge, fill=0.0, base=0,
                                                     channel_multiplier=-1),
                   reads=[("c", "wTm")], writes=[("c", "wTm")])
            spdma(vgain[:], wl("sg_v_gain", l).broadcast_to([128, 1024]), "lvg", writes=[("c", "vg")])

            sgkeys = [("ug", c, th) for c in range(8) for th in range(TH)] + \
                     [("vln", tt) for tt in range(NTT)] + [("vgt", i) for i in range(4)] + \
                     [("sqt", i) for i in range(2)]
            P.guard(sgkeys)

            def evac_u(fc, banks):
                for th in range(TH):
                    act(u_g[:, fc, th * 512:(th + 1) * 512], psb[banks[th]][:], AF.Gelu,
                        [("ps", banks[th])], [("ug", fc, th)])

            proj_fm(wv, 0, 8, evac_u)

            vi = [0]

            def evac_vsg(q, tt, ps, pk):
                i = vi[0] % 4
                vi[0] += 1
                vg = vgt[i]
                vk = ("vgt", i)
                sm0 = 128 + i * 20
                sm_stat = small[:, sm0:sm0 + 12]
                sm_mv = small[:, sm0 + 12:sm0 + 16]
                sm_rs = small[:, sm0 + 16:sm0 + 18]
                sm_ln = small[:, sm0 + 18:sm0 + 20]
                kst, kmv, krs_, kln = ("smst", i), ("smmv", i), ("smrs", i), ("smln", i)
                act(vg, ps, AF.Gelu, [pk], [vk])
                for hh in range(2):
                    dve(lambda e, hh=hh, vg=vg, sm_stat=sm_stat: e.bn_stats(
                        out=sm_stat[:, hh * 6:(hh + 1) * 6], in_=vg[:, hh * 128:(hh + 1) * 128]),
                        [vk], [kst])
                    dve(lambda e, hh=hh, sm_stat=sm_stat, sm_mv=sm_mv: e.bn_aggr(
                        out=sm_mv[:, hh * 2:(hh + 1) * 2], in_=sm_stat[:, hh * 6:(hh + 1) * 6]),
                        [kst], [kmv])
                mvv = sm_mv.rearrange("p (h c) -> p h c", c=2)
                act(sm_ln.unsqueeze(2), mvv[:, :, 1:2], AF.Ln, [kmv, ("c", "eps")], [kln],
                    bias=epsT[:], scale=1.0)
                act(sm_rs, sm_ln, AF.Exp, [kln], [krs_], scale=-0.5)
                for hh in range(2):
                    dve(lambda e, hh=hh, vg=vg, sm_mv=sm_mv, sm_rs=sm_rs: e.tensor_scalar(
                        out=vg[:, hh * 128:(hh + 1) * 128], in0=vg[:, hh * 128:(hh + 1) * 128],
                        scalar1=sm_mv[:, hh * 2:hh * 2 + 1], scalar2=sm_rs[:, hh:hh + 1],
                        op0=ALU.subtract, op1=ALU.mult),
                        [vk, kmv, krs_], [vk])
                dve(lambda e, vg=vg, q=q, tt=tt: e.tensor_tensor(
                    out=v_ln[:, tt, q * 256:(q + 1) * 256], in0=vg, in1=vgain[:, q * 256:(q + 1) * 256],
                    op=ALU.mult),
                    [vk, ("c", "vg")], [("vln", tt)])

            proj_tok(wv, 1024, 4, evac_vsg)

            mi = [0]
            for tt in range(NTT):
                th = tt // 4
                for hd in range(8):
                    r = mi[0] % 4
                    mi[0] += 1
                    pm = psb[4 + r][:, 0:128]
                    pk = ("ps", 4 + r)
                    mm(pm, v_ln[:, tt, hd * 128:(hd + 1) * 128], wTm[:, hd, :], True, False,
                       [("vln", tt), ("c", "wTm")], [pk])
                    mm(pm, ones_bf[0:1, :], bs_row[0:1, hd * 128:(hd + 1) * 128], False, True,
                       [("c", "ones"), ("c", "bs")], [pk])
                    us = u_g[:, hd, tt * 128:(tt + 1) * 128]
                    dve(lambda e, us=us, pm=pm: e.tensor_tensor(out=us, in0=pm, in1=us, op=ALU.mult),
                        [pk, ("ug", hd, th)], [("ug", hd, th)])
            mix_out_half(l, 0, u_g, "ug")
            if stop_after == ("mixA", l):
                return True

            mark("sg")
            m1keys = [("qd", c, th) for c in range(4) for th in range(TH)] + \
                     [("kd", c, th) for c in range(4) for th in range(TH)] + \
                     [("kdt", tt) for tt in range(NTT)] + [("vt", tt) for tt in range(NTT)] + \
                     [("sp", tt) for tt in range(NTT)] + [("spl", tt) for tt in range(NTT)] + [("zT",), ("mtmp", 0), ("mtmp", 1), ("mtmp2", 0),
                                                          ("mtmp2", 1), ("S", 0), ("S", 1), ("S", 2),
                                                          ("S", 3), ("S", "B"), ("stg",)]
            P.guard(m1keys)
            dve(lambda e: e.memset(zT[0:32, :], 1.0), [], [("zT",)])
            si = next_aslot()
            slz = aslot(si, "p (k c) -> p k c", k=KC)
            skz = ("A", si)
            wdma(slz[:, :, 0:16], wv[:, :, 5120:5136], [skz])
            for k in range(KC):
                for th in range(TH):
                    mm(psb[th][0:16, :], slz[:, k, 0:16], nb[:, k, th * 512:(th + 1) * 512],
                       k == 0, k == KC - 1, [skz, nbk(k, th)], [("ps", th)])
            for th in range(TH):
                act(zT[0:16, th * 512:(th + 1) * 512], psb[th][0:16, :], AF.Copy, [("ps", th)], [("zT",)])
            mark("z")
            for tt in range(NTT):
                bank = 2 + (tt % 2)
                pk = ("ps", bank)
                mm(psb[bank][:], zT[0:17, tt * 128:(tt + 1) * 128], wgate[0:17, :], True, True,
                   [("zT",), ("c", "wg")], [pk])
                t = mtmp[tt % 2]
                tk = ("mtmp", tt % 2)
                act(t, psb[bank][:], AF.Exp, [pk], [tk], scale=-1.0)
                act(t, t, AF.Ln, [tk], [tk], bias=1.0, scale=1.0)
                dve(lambda e, tt=tt, t=t: e.tensor_copy(out=sp_hi[:, tt, :], in_=t), [tk], [("sp", tt)])
                dve(lambda e, tt=tt, t=t: e.tensor_tensor(out=sp_lo[:, tt, :], in0=t, in1=sp_hi[:, tt, :],
                                                          op=ALU.subtract),
                    [tk, ("sp", tt)], [("spl", tt)])

            mark("sp")

            def make_evac_qk(dst, dkey, is_q):
                def evac(fc, banks):
                    hd = fc
                    for th in range(TH):
                        bb = 4 + th
                        for t4 in range(4):
                            tt = th * 4 + t4
                            mm(psb[bb][:, t4 * 128:(t4 + 1) * 128], sp_hi[:, tt, hd * 128:(hd + 1) * 128],
                               U_inc[:], True, False, [("sp", tt), ("c", "Ui")], [("ps", bb)])
                            mm(psb[bb][:, t4 * 128:(t4 + 1) * 128], sp_lo[:, tt, hd * 128:(hd + 1) * 128],
                               U_inc[:], False, True, [("spl", tt), ("c", "Ui")], [("ps", bb)])
                        t = mtmp[th]
                        tk = ("mtmp", th)
                        if is_q:
                            dve(lambda e, hd=hd, th=th, bb=bb: e.tensor_copy(
                                out=blast[:, hd, th * 4:(th + 1) * 4],
                                in_=psb[bb][:].rearrange("p (t c) -> p t c", c=128)[:, :, 127]),
                                [("ps", bb)], [("sm", "blast")])
                            act(t, psb[bb][:], AF.Exp, [("ps", bb)], [tk])
                            dve(lambda e, o=dst[:, hd, th * 512:(th + 1) * 512], ps=psb[banks[th]][:], t=t:
                                e.scalar_tensor_tensor(out=o, in0=ps, scalar=128.0 ** -0.5, in1=t,
                                                       op0=ALU.mult, op1=ALU.mult),
                                [("ps", banks[th]), tk], [(dkey, hd, th)])
                        else:
                            act(t, psb[bb][:], AF.Exp, [("ps", bb)], [tk], scale=-1.0)
                            dve(lambda e, o=dst[:, hd, th * 512:(th + 1) * 512], ps=psb[banks[th]][:], t=t:
                                e.tensor_tensor(out=o, in0=ps, in1=t, op=ALU.mult),
                                [("ps", banks[th]), tk], [(dkey, hd, th)])
                return evac

            proj_fm(wv, 2048, 4, make_evac_qk(qd, "qd", True))
            proj_fm(wv, 2560, 4, make_evac_qk(kd, "kd", False))
            act(dec.rearrange("p h t -> p (h t)"), blast.rearrange("p h t -> p (h t)"), AF.Exp,
                [("sm", "blast")], [("sm", "dec")])

            mark("qk")

            def evac_ktok(q, tt, ps, pk):
                pd = psb[6 + (tt % 2)][:, 0:256]
                pdk = ("ps", 6 + (tt % 2))
                mm(pd, U_str[:], sp_hi[:, tt, q * 256:(q + 1) * 256], True, False,
                   [("sp", tt), ("c", "Us")], [pdk])
                mm(pd, U_str[:], sp_lo[:, tt, q * 256:(q + 1) * 256], False, True,
                   [("spl", tt), ("c", "Us")], [pdk])
                t = mtmp[2][:, (tt % 2) * 256:(tt % 2 + 1) * 256]
                tk = ("mtmp2", tt % 2)
                act(t, pd, AF.Exp, [pdk], [tk])
                dve(lambda e, q=q, tt=tt, ps=ps, t=t: e.tensor_tensor(
                    out=kd_tok[:, tt, q * 256:(q + 1) * 256], in0=ps, in1=t, op=ALU.mult),
                    [pk, tk], [("kdt", tt)])

            proj_tok(wv, 2560, 2, evac_ktok)

            mark("ktok")

            def evac_vtok(q, tt, ps, pk):
                act(v_tok[:, tt, q * 256:(q + 1) * 256], ps, AF.Copy, [pk], [("vt", tt)])

            proj_tok(wv, 3072, 4, evac_vtok)

            mark("vtok")
            Sv = Sst[:, 0:1024].rearrange("p (h v) -> p h v", h=4)

            def state_update(tt, first):
                for hd in range(4):
                    bank = hd % 2
                    pS = psb[bank][:, 0:256]
                    pk = ("ps", bank)
                    mm(pS, kd_tok[:, tt, hd * 128:(hd + 1) * 128], v_tok[:, tt, hd * 256:(hd + 1) * 256],
                       True, True, [("kdt", tt), ("vt", tt)], [pk])
                    if first:
                        dve(lambda e, hd=hd, pS=pS: e.tensor_copy(out=Sv[:, hd, :], in_=pS),
                            [pk], [("S", hd)])
                    else:
                        dve(lambda e, hd=hd, pS=pS, tt=tt: e.scalar_tensor_tensor(
                            out=Sv[:, hd, :], in0=Sv[:, hd, :], scalar=dec[:, hd, tt:tt + 1], in1=pS,
                            op0=ALU.mult, op1=ALU.add),
                            [pk, ("S", hd), ("sm", "dec")], [("S", hd)])

            if not seq:
                for tt in range(NTT):
                    state_update(tt, tt == 0)
                dve(lambda e: e.tensor_reduce(out=Sst[:, 1024:1028], in_=blast, axis=AX.X, op=ALU.add),
                    [("sm", "blast")], [("S", "B")])
            skeys = [("S", hd) for hd in range(4)] + [("S", "B")]
            if seq:
                pass
            elif fused:
                spdma(cc_in[:, :], Sst[:, 0:PAYW], "cci", reads=skeys, writes=[("ccin",)])
                P.emit("pool", lambda e: e.collective_compute(
                    "AllGather", ALU.bypass, replica_groups=[list(range(NCORES))],
                    ins=[cc_in[:, :].opt()], outs=[cc_out[:, :].opt()]),
                    reads=[("ccin",)], writes=[("ccout",)], dma="cc", dma_inc=1)
                gsrc = cc_out
            else:
                if part == 1:
                    spdma(dout("payload", [128, PAYW])[:, :], Sst[:, 0:PAYW], "out", reads=skeys)
                P.muted = (part == 1)
                gsrc = din("gath", [NCORES * 128, PAYW])
                wv = wl("w_mix_in", l).rearrange("(k p) c -> p k c", p=128)

            P.guard([("rs", c, th) for c in range(8) for th in range(TH)])

            def evac_r(fc, banks):
                for th in range(TH):
                    act(r_s[:, fc, th * 512:(th + 1) * 512], psb[banks[th]][:], AF.Silu,
                        [("ps", banks[th])], [("rs", fc, th)])

            proj_fm(wv, 4096, 8, evac_r)

            P.guard([("Sbf", 0), ("Sbf", 1), ("stg",)] + [(n, i) for n in ("p2rs", "p2t", "osq", "scm") for i in range(2)])
            if seq:
                if cur_seg[0] == 0:
                    dve(lambda e: e.memset(Sst[:, 0:1024], 0.0), [], skeys)
                else:
                    spdma(Sst[:, 0:1024], gla_state[l], "stg", reads=[("gst", l)], writes=skeys)
            else:
                dve(lambda e: e.memset(Sst[:, 0:1024], 0.0), [], skeys)
            for j in (() if seq else (0, 1, 2, 4, 5, 6)):
                spdma(stg[:, 0:PAYW], gsrc[j * 128:(j + 1) * 128, :], "stg",
                      reads=[("ccout",)] if fused else [], writes=[("stg",)])
                selj = parT[:, 160 + j:161 + j]
                act(ea, stg[:, 1024:1028], AF.Exp, [("stg",)], [("sm", "ea")])
                dve(lambda e, selj=selj: e.tensor_scalar(out=a3, in0=ea, scalar1=-1.0, scalar2=selj,
                                                          op0=ALU.add, op1=ALU.mult),
                    [("sm", "ea"), ("c", "par")], [("sm", "a3")])
                dve(lambda e: e.tensor_scalar(out=a3, in0=a3, scalar1=1.0, scalar2=None, op0=ALU.add),
                    [("sm", "a3")], [("sm", "a3")])
                for hd in range(4):
                    sj = stg[:, hd * 256:(hd + 1) * 256]
                    dve(lambda e, sj=sj, selj=selj: e.tensor_scalar(out=sj, in0=sj, scalar1=selj,
                                                                    scalar2=None, op0=ALU.mult),
                        [("stg",), ("c", "par")], [("stg",)])
                    dve(lambda e, hd=hd, sj=sj: e.scalar_tensor_tensor(
                        out=Sv[:, hd, :], in0=Sv[:, hd, :], scalar=a3[:, hd:hd + 1], in1=sj,
                        op0=ALU.mult, op1=ALU.add),
                        [("stg",), ("sm", "a3"), ("S", hd)], [("S", hd)])

            act(S_bf[0], Sst[:, 0:1024], AF.Copy, [("S", hd) for hd in range(4)], [("Sbf", 0)])
            og0 = 144 + l * 8
            items = [(tt, hd) for tt in range(NTT) for hd in range(4)]
            nit = len(items)
            PO_BANKS = (2, 3, 7)

            def emit_sc(i):
                tt, hd = items[i]
                th = tt // 4
                tsl = slice(tt * 128, (tt + 1) * 128)
                i2 = i % 2
                psc = psb[i2][:, 0:128]
                kpsc = ("ps", i2)
                mm(psc, kd[:, hd, tsl], qd[:, hd, tsl], True, True,
                   [("kd", hd, th), ("qd", hd, th)], [kpsc])
                sc = scm[i2]
                dve(lambda e, sc=sc, psc=psc: e.tensor_tensor(out=sc, in0=psc, in1=maskT[:], op=ALU.mult),
                    [kpsc, ("c", "mk")], [("scm", i2)])

            def emit_o(i):
                tt, hd = items[i]
                th = tt // 4
                tsl = slice(tt * 128, (tt + 1) * 128)
                i2 = i % 2
                sc = scm[i2]
                ksc = ("scm", i2)
                pb = PO_BANKS[i % 3]
                po = psb[pb][:, 0:256]
                kpo = ("ps", pb)
                cur = tt % 2
                for vc in range(2):
                    vs = slice(hd * 256 + vc * 128, hd * 256 + (vc + 1) * 128)
                    mm(po[:, vc * 128:(vc + 1) * 128], v_tok[:, tt, vs], sc, True, False,
                       [("vt", tt), ksc], [kpo])
                    mm(po[:, vc * 128:(vc + 1) * 128], S_bf[cur][:, vs], qd[:, hd, tsl], False, True,
                       [("Sbf", cur), ("qd", hd, th)], [kpo])
                act(p2b[i2], po, AF.Square, [kpo], [("osq", i2)])

            def emit_ss(i):
                tt, hd = items[i]
                th = tt // 4
                tsl = slice(tt * 128, (tt + 1) * 128)
                i2 = i % 2
                pb = PO_BANKS[i % 3]
                po = psb[pb][:, 0:256]
                kpo = ("ps", pb)
                osq = p2b[i2]
                kosq = ("osq", i2)
                pss = psb[4 + i2][:, 0:128]
                kpss = ("ps", 4 + i2)
                mm(pss, ones_bf[:], osq[:, 0:128], True, False, [kosq, ("c", "ones")], [kpss])
                mm(pss, ones_bf[:], osq[:, 128:256], False, True, [kosq], [kpss])
                rs = p2f[i2]
                krs = ("p2rs", i2)
                act(rs, pss, AF.Ln, [kpss, ("c", "eps")], [krs], bias=epsT[:], scale=1.0 / 256.0)
                act(rs, rs, AF.Exp, [krs], [krs], scale=-0.5)
                for vc in range(2):
                    c = hd * 2 + vc
                    tq = p2f[2 + vc]
                    ktq = ("p2t", vc)
                    dve(lambda e, tq=tq, po=po, vc=vc, c=c, rs=rs: e.scalar_tensor_tensor(
                        out=tq, in0=po[:, vc * 128:(vc + 1) * 128],
                        scalar=parT[:, og0 + c:og0 + c + 1], in1=rs, op0=ALU.mult, op1=ALU.mult),
                        [kpo, krs, ("c", "par")], [ktq])
                    ys = r_s[:, c, tsl]
                    dve(lambda e, ys=ys, tq=tq: e.tensor_tensor(out=ys, in0=tq, in1=ys, op=ALU.mult),
                        [ktq, ("rs", c, th)], [("rs", c, th)])

            def emit_upd(tt):
                if not (tt < NTT - 1 or (seq and cur_seg[0] < NSEG - 1)):
                    return
                for hd in range(4):
                    pS = psb[6][:, 0:256]
                    pk = ("ps", 6)
                    mm(pS, kd_tok[:, tt, hd * 128:(hd + 1) * 128], v_tok[:, tt, hd * 256:(hd + 1) * 256],
                       True, True, [("kdt", tt), ("vt", tt)], [pk])
                    dve(lambda e, hd=hd, pS=pS, tt=tt: e.scalar_tensor_tensor(
                        out=Sv[:, hd, :], in0=Sv[:, hd, :], scalar=dec[:, hd, tt:tt + 1], in1=pS,
                        op0=ALU.mult, op1=ALU.add),
                        [pk, ("S", hd), ("sm", "dec")], [("S", hd)])
                if tt < NTT - 1:
                    act(S_bf[(tt + 1) % 2], Sst[:, 0:1024], AF.Copy, [("S", hd) for hd in range(4)],
                        [("Sbf", (tt + 1) % 2)])

            for step in range(nit + 2):
                if step < nit:
                    emit_sc(step)
                if 1 <= step <= nit:
                    emit_o(step - 1)
                if 2 <= step <= nit + 1:
                    emit_ss(step - 2)
                if step < nit and items[step][1] == 0:
                    emit_upd(items[step][0])
            if seq and cur_seg[0] < NSEG - 1:
                gtok = spdma(gla_state[l], Sst[:, 0:1024], "gst", reads=[("S", hd) for hd in range(4)],
                             writes=[("gst", l)])
                P.guard([("B", 0), ("B", 1)], [gtok])
            mix_out_half(l, 1024, r_s, "rs")
            P.muted = False
            return False

        def ple_stage(l):
            P.guard([("pTb",)])
            sg0 = cur_seg[0] * NT
            d_pT = wl("pT", l).rearrange("(k p) t -> p k t", p=128)[:, :, sg0:sg0 + NT]
            P.emit("pool", lambda e: e.dma_start(out=pTb[:], in_=d_pT), writes=[("pTb",)], dma="pin")
            si = next_aslot()
            wp = aslot(si, "p (k c) -> p k c", k=2)
            wpk = ("A", si)
            wdma(wp[:], wl("w_ple_proj", l).rearrange("(k p) c -> p k c", p=128), [wpk])
            wv = wl("w_ple_gate", l).rearrange("(k p) c -> p k c", p=128)

            def evac(fc, banks):
                pb = (4, 5) if fc % 2 == 0 else (6, 7)
                for k2 in range(2):
                    for th in range(TH):
                        mm(psb[pb[th]][:], wp[:, k2, fc * 128:(fc + 1) * 128], pTb[:, k2, th * 512:(th + 1) * 512],
                           k2 == 0, k2 == 1, [wpk, ("pTb",)], [("ps", pb[th])])
                for th in range(TH):
                    i = (fc * 2 + th) % 4
                    t = ftmp[i]
                    tk = ("ftmp", i)
                    act(t, psb[banks[th]][:], AF.Sigmoid, [("ps", banks[th])], [tk])
                    dve(lambda e, t=t, ps=psb[pb[th]][:]: e.tensor_tensor(out=t, in0=ps, in1=t, op=ALU.mult),
                        [("ps", pb[th]), tk], [tk])
                    hs = h[:, fc, th * 512:(th + 1) * 512]
                    dve(lambda e, hs=hs, t=t: e.tensor_tensor(out=hs, in0=hs, in1=t, op=ALU.add),
                        [tk, hk(fc, th)], [hk(fc, th)])

            astate["keep"] = si
            proj_fm(wv, 0, KC, evac)
            astate["keep"] = None

        ffn_keys = [("actb", s, fc, th) for s in range(2) for fc in range(GC) for th in range(TH)] + \
                   [("ftmp", i) for i in range(4)]

        def save_state():
            toks = P.snapshot()
            hv = dout("h_out", [128, KC * NT])
            for k in range(KC):
                spdma(hv[:, k * NT:(k + 1) * NT], h[:, k, :], "out", extra=toks)
            spdma(dout("nb_out", [128, KC * NT], BF16)[:, :], nb[:].rearrange("p k t -> p (k t)"), "out",
                  extra=toks)
            spdma(dout("r1_out", [128, 20480], BF16)[:, :], R1[:], "out", extra=toks)
            spdma(dout("sm_out", [128, 256])[:, :], small[:], "out", extra=toks)

        def restore_state():
            hv = din("h_sv", [128, KC * NT])
            for k in range(KC):
                spdma(h[:, k, :], hv[:, k * NT:(k + 1) * NT], "rst")
            spdma(nb[:].rearrange("p k t -> p (k t)"), din("nb_sv", [128, KC * NT], BF16)[:, :], "rst")
            spdma(R1[:], din("r1_sv", [128, 20480], BF16)[:, :], "rst")
            spdma(small[:], din("sm_sv", [128, 256])[:, :], "rst")
            total = P.dcnt["rst"]
            for e in P.ops:
                P.ops[e].append(([("rst", total)], None, None))
                P.waited[e]["rst"] = total

        def layer_a(l, part):
            P.guard(ffn_keys + [("B", 0), ("B", 1)])
            norm_stage((l * 4 + 0) * 16)
            ffn_stage(wl("w_ffn1_in", l), wl("w_ffn1_out", l))
            norm_stage((l * 4 + 1) * 16)
            mixer_stage(l, part)

        def layer_b(l):
            P.guard(ffn_keys + [("B", 0), ("B", 1)])
            norm_stage((l * 4 + 2) * 16)
            ffn_stage(wl("w_ffn2_in", l), wl("w_ffn2_out", l))
            norm_stage((l * 4 + 3) * 16)
            ple_stage(l)

        def store_out(seg=0):
            ov = dout("outT", [D, SEQT]).rearrange("(k p) t -> p k t", p=128)
            for k in range(KC):
                spdma(ov[:, k, seg * NT:(seg + 1) * NT], h[:, k, :], "out", reads=[hk(k, 0), hk(k, 1)])
            for k in range(KC):
                for th in range(TH):
                    P.res[hk(k, th)][1]["out"] = P.dcnt["out"]

        if seq:
            for seg in range(NSEG):
                cur_seg[0] = seg
                load_x(seg)
                for l in range(DEPTH):
                    layer_a(l, 0)
                    layer_b(l)
                norm_stage(8 * 16, final=True)
                store_out(seg)
        elif fused:
            load_x()
            for l in range(DEPTH):
                layer_a(l, 0)
                layer_b(l)
            norm_stage(8 * 16, final=True)
            store_out()
        elif phase == 0:
            load_x()
            layer_a(0, 1)
            save_state()
        elif phase == 1:
            restore_state()
            mixer_stage(0, 2)
            layer_b(0)
            layer_a(1, 1)
            save_state()
        else:
            restore_state()
            mixer_stage(1, 2)
            layer_b(1)
            norm_stage(8 * 16, final=True)
            store_out()
        P.ops["sp"].append(([("out", P.dcnt["out"])], None, None))

        def runner(name):
            def body(eng):
                for waits, fn, inc in P.ops[name]:
                    for (k, v) in waits:
                        eng.wait_ge(sems[k], v)
                    if fn is None:
                        continue
                    ins = fn(eng)
                    if inc is not None:
                        ins.then_inc(sems[inc[0]], inc[1])
            return body

        block.sync(runner("sp"))
        block.gpsimd(runner("pool"))
        block.scalar(runner("act"))
        block.vector(runner("dve"))
        block.tensor(runner("pe"))
    return nc, list(needed.keys()), list(outs_decl.keys())


_CACHE = {}
MODE = "phased"


def _make_provider(inputs):
    x = np.asarray(inputs["x"], dtype=np.float32).reshape(-1, D)
    p = np.asarray(inputs["p"], dtype=np.float32).reshape(DEPTH, -1, 256)
    gains = []
    for l in range(DEPTH):
        for n in ("ffn1_norm", "mix_norm", "ffn2_norm", "ple_norm"):
            gains.append(np.asarray(inputs[n], dtype=np.float32)[l].reshape(KC, 128).T)
    gains.append(np.asarray(inputs["final_norm"], dtype=np.float32).reshape(KC, 128).T)
    og = [np.asarray(inputs["gla_o_gain"], dtype=np.float32)[l].reshape(8, 128).T for l in range(DEPTH)]
    shared = {}

    def get(name, c, extra):
        if name in extra:
            return extra[name][c]
        if name == "xT":
            return np.ascontiguousarray(x[c * NT:(c + 1) * NT].T)
        if name == "par":
            sel = np.zeros((128, 8), np.float32)
            for j in range(NCORES):
                if j // 4 == c // 4 and j < c:
                    sel[:, j] = 1.0
            return np.ascontiguousarray(np.concatenate(gains + og + [sel], axis=1).astype(np.float32))
        base, l = name.rsplit("_", 1)
        l = int(l)
        if base == "pT":
            return np.ascontiguousarray(p[l, c * NT:(c + 1) * NT, :].T)
        if name not in shared:
            if base == "sg_wT":
                a = np.asarray(inputs["sg_w"], dtype=np.float32)[l].transpose(2, 0, 1).reshape(128, 1024)
            elif base in ("sg_b", "sg_v_gain", "gla_b_gate"):
                a = np.asarray(inputs[base], dtype=np.float32)[l].reshape(1, -1)
            else:
                a = np.asarray(inputs[base], dtype=np.float32)[l]
            shared[name] = np.ascontiguousarray(a)
        return shared[name]

    return get


def kernel(**inputs):
    get = _make_provider(inputs)
    cores = list(range(NCORES))
    if MODE == "seq":
        if "seq" not in _CACHE:
            _CACHE["seq"] = build_program(seq=True)
        nc, needed, _ = _CACHE["seq"]
        x = np.asarray(inputs["x"], dtype=np.float32)
        p = np.asarray(inputs["p"], dtype=np.float32)
        in_maps = []
        for b in range(2):
            m = {}
            for n in needed:
                if n == "xT":
                    m[n] = np.ascontiguousarray(x[b].T)
                elif n.startswith("pT_"):
                    m[n] = np.ascontiguousarray(p[int(n[3:]), b].T)
                else:
                    m[n] = get(n, 0, {})
            in_maps.append(m)
        res = run_bass_kernel_spmd(nc, in_maps, core_ids=[0, 1])
        out = np.stack([np.asarray(r["outT"]).T for r in res.results], axis=0)
        return np.ascontiguousarray(out.astype(np.float32))
    if MODE == "fused":
        if "fused" not in _CACHE:
            _CACHE["fused"] = build_program()
        nc, needed, _ = _CACHE["fused"]
        in_maps = [{n: get(n, c, {}) for n in needed} for c in cores]
        res = run_bass_kernel_spmd(nc, in_maps, core_ids=cores)
    else:
        extra = {}
        for ph in range(3):
            key = ("phase", ph)
            if key not in _CACHE:
                _CACHE[key] = build_program(phase=ph)
            nc, needed, _ = _CACHE[key]
            in_maps = [{n: get(n, c, extra) for n in needed} for c in cores]
            res = run_bass_kernel_spmd(nc, in_maps, core_ids=cores)
            if ph < 2:
                extra = {k + "_sv": [np.asarray(res.results[c][k + "_out"]) for c in cores]
                         for k in ("h", "nb", "r1", "sm")}
                gath = np.ascontiguousarray(
                    np.concatenate([np.asarray(res.results[c]["payload"]) for c in cores], axis=0))
                extra["gath"] = [gath] * NCORES
    outs = [np.asarray(r["outT"]).T for r in res.results]
    out = np.concatenate(outs, axis=0).reshape(2, 4096, D).astype(np.float32)
    return out
```
